# Optimizing a Trainium2 kernel written in Bass

```python
import math
import jax, jax.numpy as jnp
from jax import lax
import numpy as np

D_MODEL = 1024
BATCH = 4
SEQ = 4096
DEPTH = 1

D_MIX = D_MODEL
W_SSM = (3 * D_MIX) // 8
W_CONV = (3 * D_MIX) // 8
W_MEM = D_MIX // 4
SSM_GROUP = 16
SSM_GROUPS = W_SSM // SSM_GROUP
SSM_STATE = 64
CONV_WIDTH = 3
MEM_HEADS = 4
MEM_HEAD_DIM = W_MEM // MEM_HEADS
N_MEM = 256
NORM_EPS = 1e-6
DT_MIN = 1e-3
DT_MAX = 1e-1
IN_SPLITS = [W_SSM, W_SSM, W_CONV, W_CONV, W_CONV, W_CONV, W_MEM, W_MEM]
D_IN = sum(IN_SPLITS)

kernel_name = "hymba_s5_shortconv_memxattn_layer"


def _rmsnorm(x, g):
    xf = x.astype(jnp.float32)
    y = xf * lax.rsqrt(jnp.mean(xf * xf, axis=-1, keepdims=True) + NORM_EPS)
    return (y * g.astype(jnp.float32)).astype(x.dtype)


def _s5_group(u, lam_re, lam_im, b_re, b_im, c_re, c_im, d, log_dt, w_glu, b_glu):
    bsz, seqlen, _ = u.shape
    f32 = jnp.float32
    uf = u.astype(f32)
    ug = uf.reshape(bsz, seqlen, SSM_GROUPS, SSM_GROUP)
    lam = lax.complex(lam_re.astype(f32), lam_im.astype(f32))
    dt = jnp.exp(log_dt.astype(f32))[:, None]
    lam_bar = jnp.exp(lam * dt)
    b_bar = ((lam_bar - 1.0) / lam)[..., None] * lax.complex(b_re.astype(f32), b_im.astype(f32))
    bu = lax.complex(jnp.einsum('blgh,gph->blgp', ug, jnp.real(b_bar)),
                     jnp.einsum('blgh,gph->blgp', ug, jnp.imag(b_bar)))
    a = jnp.broadcast_to(lam_bar, bu.shape)

    def combine(left, right):
        a_l, b_l = left
        a_r, b_r = right
        return a_l * a_r, a_r * b_l + b_r

    _, s = lax.associative_scan(combine, (a, bu), axis=1)
    y = (jnp.einsum('blgp,ghp->blgh', jnp.real(s), c_re.astype(f32))
         - jnp.einsum('blgp,ghp->blgh', jnp.imag(s), c_im.astype(f32)))
    y = y.reshape(bsz, seqlen, W_SSM) + d.astype(f32) * uf
    g = jax.nn.gelu(y)
    out = g * jax.nn.sigmoid(g @ w_glu.astype(f32) + b_glu.astype(f32))
    return out.astype(u.dtype)


def _short_conv_group(x_in, gate_b, gate_c, conv_w, conv_b):
    u = gate_c * x_in
    up = jnp.pad(u, ((0, 0), (CONV_WIDTH - 1, 0), (0, 0)))
    seqlen = x_in.shape[1]
    y = conv_b + sum(conv_w[k] * up[:, k:k + seqlen] for k in range(CONV_WIDTH))
    return gate_b * y


def _memory_xattn_group(q, mem_n, w_mem_kv):
    bsz, seqlen, _ = q.shape
    kv = mem_n @ w_mem_kv
    k, v = jnp.split(kv, 2, axis=-1)
    qh = q.reshape(bsz, seqlen, MEM_HEADS, MEM_HEAD_DIM)
    kh = k.reshape(bsz, N_MEM, MEM_HEADS, MEM_HEAD_DIM)
    vh = v.reshape(bsz, N_MEM, MEM_HEADS, MEM_HEAD_DIM)
    scores = jnp.einsum('blhd,bmhd->bhlm', qh, kh).astype(jnp.float32) / math.sqrt(MEM_HEAD_DIM)
    p = jax.nn.softmax(scores, axis=-1).astype(q.dtype)
    o = jnp.einsum('bhlm,bmhd->blhd', p, vh)
    return o.reshape(bsz, seqlen, W_MEM)


def setup_inputs(seed: int = 0) -> dict:
    key = jax.random.key(seed)
    ks = jax.random.split(key, 24)
    f32 = jnp.float32
    n = lambda k, shape, scale: scale * jax.random.normal(k, shape, f32)
    x = jax.random.normal(ks[0], (BATCH, SEQ, D_MODEL), f32)
    mem = jax.random.normal(ks[1], (BATCH, N_MEM, D_MODEL), f32)
    norm_g = 1.0 + n(ks[2], (DEPTH, D_MODEL), 0.02)
    w_in = n(ks[3], (DEPTH, D_MODEL, D_IN), D_MODEL ** -0.5)
    idx = jnp.arange(SSM_STATE, dtype=f32)
    ssm_lam_re = -0.5 + n(ks[4], (DEPTH, SSM_GROUPS, SSM_STATE), 0.01)
    ssm_lam_im = math.pi * idx + n(ks[5], (DEPTH, SSM_GROUPS, SSM_STATE), 0.01)
    ssm_b_re = n(ks[6], (DEPTH, SSM_GROUPS, SSM_STATE, SSM_GROUP), (2 * SSM_GROUP) ** -0.5)
    ssm_b_im = n(ks[7], (DEPTH, SSM_GROUPS, SSM_STATE, SSM_GROUP), (2 * SSM_GROUP) ** -0.5)
    ssm_c_re = n(ks[8], (DEPTH, SSM_GROUPS, SSM_GROUP, SSM_STATE), (2 * SSM_STATE) ** -0.5)
    ssm_c_im = n(ks[9], (DEPTH, SSM_GROUPS, SSM_GROUP, SSM_STATE), (2 * SSM_STATE) ** -0.5)
    ssm_d = n(ks[10], (DEPTH, W_SSM), 1.0)
    ssm_log_dt = jax.random.uniform(ks[11], (DEPTH, SSM_GROUPS), f32,
                                    math.log(DT_MIN), math.log(DT_MAX))
    w_glu = n(ks[12], (DEPTH, W_SSM, W_SSM), W_SSM ** -0.5)
    b_glu = n(ks[13], (DEPTH, W_SSM), 0.02)
    conv_w = n(ks[14], (DEPTH, CONV_WIDTH, W_CONV), CONV_WIDTH ** -0.5)
    conv_b = n(ks[15], (DEPTH, W_CONV), 0.02)
    mem_norm_g = 1.0 + n(ks[16], (DEPTH, D_MODEL), 0.02)
    w_mem_kv = n(ks[17], (DEPTH, D_MODEL, 2 * W_MEM), D_MODEL ** -0.5)
    w_out = n(ks[18], (DEPTH, D_MIX, D_MODEL), D_MIX ** -0.5)
    final_norm_g = 1.0 + n(ks[19], (D_MODEL,), 0.02)
    return {"x": x, "mem": mem, "norm_g": norm_g, "w_in": w_in,
            "ssm_lam_re": ssm_lam_re, "ssm_lam_im": ssm_lam_im,
            "ssm_b_re": ssm_b_re, "ssm_b_im": ssm_b_im,
            "ssm_c_re": ssm_c_re, "ssm_c_im": ssm_c_im,
            "ssm_d": ssm_d, "ssm_log_dt": ssm_log_dt,
            "w_glu": w_glu, "b_glu": b_glu,
            "conv_w": conv_w, "conv_b": conv_b,
            "mem_norm_g": mem_norm_g, "w_mem_kv": w_mem_kv,
            "w_out": w_out, "final_norm_g": final_norm_g}


def reference(x, mem, norm_g, w_in, ssm_lam_re, ssm_lam_im, ssm_b_re, ssm_b_im,
              ssm_c_re, ssm_c_im, ssm_d, ssm_log_dt, w_glu, b_glu, conv_w, conv_b,
              mem_norm_g, w_mem_kv, w_out, final_norm_g):
    split_idx = list(np.cumsum(IN_SPLITS)[:-1])
    h = x
    for l in range(DEPTH):
        hn = _rmsnorm(h, norm_g[l])
        proj = hn @ w_in[l]
        (a_in, a_gate, c_in, c_b, c_c, c_gate, q, q_gate) = jnp.split(proj, split_idx, axis=-1)
        y_ssm = _s5_group(a_in, ssm_lam_re[l], ssm_lam_im[l], ssm_b_re[l], ssm_b_im[l],
                          ssm_c_re[l], ssm_c_im[l], ssm_d[l], ssm_log_dt[l],
                          w_glu[l], b_glu[l]) * jax.nn.silu(a_gate)
        y_conv = _short_conv_group(c_in, c_b, c_c, conv_w[l], conv_b[l]) * jax.nn.silu(c_gate)
        mem_n = _rmsnorm(mem, mem_norm_g[l])
        y_mem = _memory_xattn_group(q, mem_n, w_mem_kv[l]) * jax.nn.silu(q_gate)
        y = jnp.concatenate([y_ssm, y_conv, y_mem], axis=-1)
        h = h + y @ w_out[l]
    return _rmsnorm(h, final_norm_g)
```

```python
from contextlib import ExitStack

import numpy as np
import concourse.bass as bass
import concourse.mybir as mybir
from concourse.bass_utils import run_bass_kernel_spmd

F32 = mybir.dt.float32
BF16 = mybir.dt.bfloat16
AF = mybir.ActivationFunctionType
ALU = mybir.AluOpType


class _Op:
    __slots__ = ("eng", "fn", "deps", "signaled", "seq", "dma_group", "dma_val", "idx")

    def __init__(self, eng, fn):
        self.eng = eng
        self.fn = fn
        self.deps = []
        self.signaled = False
        self.seq = 0
        self.dma_group = None
        self.dma_val = 0
        self.idx = 0


class Sched:
    ENG = ("pe", "act", "dve", "pool", "sp")

    def __init__(self, nc):
        self.nc = nc
        self.ops = []
        self.byeng = {e: [] for e in self.ENG}
        self.res = {}
        self.groups = {}
        self.final = []
        self.pending = {}

    def _track(self, op, reads, writes):
        deps = {}

        def add(o, kind):
            if o is None or o is op:
                return
            if o.dma_group is not None and o.dma_group == op.dma_group and self.groups[o.dma_group]["bulk"]:
                return
            if o.eng == op.eng and o.dma_group is None:
                if op.eng == "pe":
                    return
            deps[o.idx] = o

        for r in reads:
            st = self.res.get(r)
            if st is not None:
                add(st[0], "raw")
        for w in writes:
            st = self.res.get(w)
            if st is not None:
                add(st[0], "waw")
                for rd in st[1]:
                    add(rd, "war")
        for r in reads:
            st = self.res.setdefault(r, [None, []])
            st[1].append(op)
        for w in writes:
            self.res[w] = [op, []]
        op.deps = list(deps.values())
        for o in op.deps:
            o.signaled = True

    def op(self, eng, fn, reads=(), writes=(), _grp=None):
        o = _Op(eng, fn)
        o.dma_group = _grp
        o.idx = len(self.ops)
        self.ops.append(o)
        self.byeng[eng].append(o)
        self._track(o, reads, writes)
        pend = self.pending.pop(eng, None)
        if pend:
            have = {d.idx for d in o.deps}
            for d in pend:
                if d.idx not in have and not (d.eng == eng and d.dma_group is None):
                    o.deps.append(d)
                    d.signaled = True
        return o

    def barrier(self):
        last = []
        for e in ("pe", "act", "dve", "pool"):
            ops = [o for o in self.byeng[e] if o.dma_group is None]
            if ops:
                last.append(ops[-1])
        seen = {}
        for o in self.ops:
            if o.dma_group is not None:
                seen[o.dma_group] = o
        last += list(seen.values())
        self.pending = {e: list(last) for e in self.ENG}

    def dma(self, eng, out, in_, reads=(), writes=(), group="misc", bulk=False, **kw):
        g = self.groups.setdefault(group, {"n": 0, "bulk": bulk})
        g["n"] += 1
        o = self.op(eng, lambda e: e.dma_start(out=out, in_=in_, **kw), reads, writes, _grp=group)
        o.dma_val = 16 * g["n"]
        return o

    def final_wait(self, group):
        self.final.append(group)

    def finish(self):
        nc = self.nc
        with ExitStack() as st:
            sems = {}
            for e in ("pe", "act", "dve", "pool"):
                sems[e] = st.enter_context(nc.semaphore("s_" + e))
            for gname in self.groups:
                sems["g_" + gname] = st.enter_context(nc.semaphore("g_" + gname))
            for e in ("pe", "act", "dve", "pool"):
                c = 0
                for o in self.byeng[e]:
                    if o.dma_group is None and o.signaled:
                        c += 1
                        o.seq = c

            def token(o):
                if o.dma_group is not None:
                    g = self.groups[o.dma_group]
                    v = 16 * g["n"] if g["bulk"] else o.dma_val
                    return "g_" + o.dma_group, v
                return o.eng, o.seq

            block = st.enter_context(nc.Block())

            def run(eng_name):
                def body(e):
                    waited = {}
                    for o in self.byeng[eng_name]:
                        for d in o.deps:
                            k, v = token(d)
                            if waited.get(k, 0) < v:
                                e.wait_ge(sems[k], v)
                                waited[k] = v
                        ins = o.fn(e)
                        if o.dma_group is not None:
                            ins.then_inc(sems["g_" + o.dma_group], 16)
                        elif o.signaled:
                            ins.then_inc(sems[eng_name], 1)
                    if eng_name == "sp":
                        for gname in self.final:
                            e.wait_ge(sems["g_" + gname], 16 * self.groups[gname]["n"])
                return body

            block.tensor(run("pe"))
            block.scalar(run("act"))
            block.vector(run("dve"))
            block.gpsimd(run("pool"))
            block.sync(run("sp"))


G, P = 24, 64
STAGE = 9
D, DIN, NT = 1024, 2816, 2048
EPS = 1e-6


def build_nc(do_ssm=True, stage=9):
    nc = bass.Bass("TRN2", target_bir_lowering=False)
    S = Sched(nc)

    def dram(name, shape, kind="ExternalInput"):
        return nc.dram_tensor(name, list(shape), F32, kind=kind).ap()

    xs = dram("xs", [NT, D]); xp = dram("xp", [NT, D]); xh = dram("xh", [128, D])
    mem = dram("mem", [256, D])
    w_in = dram("w_in", [D, DIN]); w_out = dram("w_out", [D, D])
    w_glu = dram("w_glu", [384, 384]); w_kv = dram("w_kv", [D, 512])
    gin = dram("gin", [128, 8]); gmem = dram("gmem", [128, 8]); gfin = dram("gfin", [128, D])
    bglu = dram("bglu", [128, 3]); cw = dram("cw", [128, 9]); cbias = dram("cbias", [128, 3])
    lre = dram("lre", [128, G]); lim = dram("lim", [128, G]); ldt = dram("ldt", [128, G])
    bre = dram("bre", [128, G * 16]); bim = dram("bim", [128, G * 16])
    cre = dram("cre", [128, G * 16]); cim = dram("cim", [128, G * 16])
    dcol = dram("dcol", [128, G])
    cident = dram("ident", [128, 128]); cmask = dram("mask", [128, 128]); cperm = dram("perm", [128, 128])
    csel = dram("sel", [128, 6])
    out = dram("out", [NT, D], kind="ExternalOutput")

    A = nc.alloc_sbuf_tensor
    win = A("win", [128, 8, DIN], BF16); wout = A("wout", [128, 8, D], BF16)
    wglu = A("wglu", [128, 3, 384], BF16)
    kT = A("kT", [128, 2, 256], BF16); vv = A("vv", [128, 2, 256], BF16)
    identf = A("identf", [128, 128], F32); identb = A("identb", [128, 128], BF16)
    maskf = A("maskf", [128, 128], F32); permf = A("permf", [128, 128], F32); permb = A("permb", [128, 128], BF16)
    onesb = A("onesb", [128, 64], BF16)
    sel = A("sel_sb", [128, 6], F32)
    gfs = A("gfs", [128, D], F32)
    prm = A("prm", [128, 32], F32)
    cws = A("cws", [128, 9], F32); cdiag = A("cdiag", [128, 9, 128], BF16)
    MI = A("MI", [128, G, 128], BF16); Wl = A("Wl", [128, G, 128], BF16)
    Wls = A("Wls", [128, G, 128], BF16); Wc = A("Wc", [128, G, 128], BF16)
    ECt = A("ECt", [128, G, 128], BF16); ESp = A("ESp", [128, G, 128], BF16)
    rho8 = A("rho8", [128, G], F32); send = A("send", [128, G], F32)
    stat = A("stat", [128, 16], F32); stat2 = A("stat2", [128, 12], F32); mhalf = A("mhalf", [128, 1], F32); epsc = A("epsc", [128, 1], F32)
    pA = nc.alloc_psum_tensor("pA", [128, 1024], F32); pB = nc.alloc_psum_tensor("pB", [128, 1024], F32)
    pC = nc.alloc_psum_tensor("pC", [128, 1024], F32)
    pT0 = nc.alloc_psum_tensor("pT0", [128, 1024], BF16); pT1 = nc.alloc_psum_tensor("pT1", [128, 1024], BF16)

    op = S.op
    TT = lambda eng, o, a, b, alu, r, w: op(eng, lambda e: e.tensor_tensor(o, a, b, alu), r, w)
    TS = lambda eng, o, a, s1, s2, o0, o1, r, w: op(eng, lambda e: e.tensor_scalar(o, a, s1, s2, o0, o1), r, w)
    STT = lambda eng, o, a, s, b, o0, o1, r, w: op(eng, lambda e: e.scalar_tensor_tensor(o, a, s, b, o0, o1), r, w)
    ACT = lambda o, a, f, r, w, **kw: op("act", lambda e: e.activation(o, a, f, **kw), r, w)
    CP = lambda eng, o, a, r, w: (op(eng, lambda e: e.tensor_copy(o, a), r, w) if eng != "act"
                                  else op(eng, lambda e: e.activation(o, a, AF.Copy), r, w))
    MM = lambda o, l, rh, st, sp, r, w: op("pe", lambda e: e.matmul(o, l, rh, start=st, stop=sp), r, w)
    TR = lambda o, i, idn, r, w: op("pe", lambda e: e.transpose(o, i, idn), r, w)

    ldgrp = ["c"]

    def ld(dst, src, name):
        S.dma("sp", dst, src, writes=[name], group=ldgrp[0], bulk=True)
    ld(identf[:], cident, "identf"); ld(maskf[:], cmask, "maskf"); ld(permf[:], cperm, "permf")
    ld(sel[:], csel, "sel"); ld(gfs[:], gfin, "gfs")
    ld(prm[:, 0:8], gin, "prm"); ld(prm[:, 8:16], gmem, "prm"); ld(prm[:, 16:19], bglu, "prm")
    ld(prm[:, 22:25], cbias, "prm"); ld(cws[:], cw, "cws")
    CP("dve", identb[:], identf[:], ["identf"], ["identb"])
    CP("dve", permb[:], permf[:], ["permf"], ["permb"])
    op("dve", lambda e: e.memset(onesb[:], 1.0), [], ["onesb"])
    op("dve", lambda e: e.memset(mhalf[:], -0.5), [], ["mhalf"])
    op("dve", lambda e: e.memset(epsc[:], EPS), [], ["epsc"])
    op("dve", lambda e: e.memset(send[:], 0.0), [], ["send00", "send01", "send10", "send11", "send20", "send21"])
    TS("dve", prm[:, 19:22], prm[:, 16:19], 0.5, None, ALU.mult, ALU.bypass, ["prm"], ["prm"])
    for i in range(9):
        TS("dve", cdiag[:, i, :], identf[:], cws[:, i:i + 1], None, ALU.mult, ALU.bypass,
           ["identf", "cws"], ["cdiag"])

    NXB = 4
    xbuf = [A(f"xb{i}", [128, D], F32) for i in range(NXB)]
    xsbs = [A(f"xsb{i}", [128, D], BF16) for i in range(2)]
    junk1 = A("junk1", [128, 2], BF16)
    junkD = junk1[:, 0:1].to_broadcast([128, D])
    tilectr = [0]

    def rstd_from(ssq_ap, v_ap, r_ap, names):
        TS("dve", v_ap, ssq_ap, 1.0 / D, EPS, ALU.mult, ALU.add, [names[0]], [names[1]])
        ACT(v_ap, v_ap, AF.Sqrt, [names[1]], [names[1]])
        op("dve", lambda e: e.reciprocal(r_ap, v_ap), [names[1]], [names[2]])

    def pre_a(src):
        i = tilectr[0]; tilectr[0] += 1
        sl = i % NXB; st_ = i % 4
        S.dma("sp", xbuf[sl][:], src, writes=[f"xb{sl}"], group=f"xb{sl}")
        ACT(junkD, xbuf[sl][:], AF.Square, [f"xb{sl}"], ["junk1", f"ssq{st_}"], accum_out=stat[:, st_:st_ + 1])
        return i

    act_only = [False]

    def pre_b(i):
        sl = i % NXB; st_ = i % 4; xn = f"xb{sl}"
        nm = [f"ssq{st_}", f"v{st_}", f"rs{st_}"]
        xs_ = xsbs[i % 2]; xsn = f"xsb{i % 2}"
        if act_only[0]:
            ACT(stat[:, 4 + st_:5 + st_], stat[:, st_:st_ + 1], AF.Ln, [nm[0], "epsc"], [nm[1]], scale=1.0 / D, bias=epsc[:, 0:1])
            ACT(stat[:, 8 + st_:9 + st_], stat[:, 4 + st_:5 + st_], AF.Exp, [nm[1]], [nm[2]], scale=-0.5)
            ACT(xs_[:], xbuf[sl][:], AF.Copy, [xn, nm[2]], [xsn], scale=stat[:, 8 + st_:9 + st_])
        else:
            ACT(stat[:, 4 + st_:5 + st_], stat[:, st_:st_ + 1], AF.Ln, [nm[0], "epsc"], [nm[1]], scale=1.0 / D, bias=epsc[:, 0:1])
            ACT(stat[:, 8 + st_:9 + st_], stat[:, 4 + st_:5 + st_], AF.Exp, [nm[1]], [nm[2]], scale=-0.5)
            TS("dve", xs_[:], xbuf[sl][:], stat[:, 8 + st_:9 + st_], None, ALU.mult, ALU.bypass, [xn, nm[2]], [xsn])
        pt = pT0 if i % 2 == 0 else pT1; pn = "pT0" if i % 2 == 0 else "pT1"
        for k in range(8):
            TR(pt[:, k * 128:(k + 1) * 128], xs_[:, k * 128:(k + 1) * 128], identb[:], [xsn, "identb"], [pn])

    def pre_c(i, dst3, dst_res):
        pt = pT0 if i % 2 == 0 else pT1; pn = "pT0" if i % 2 == 0 else "pT1"
        CP("act" if act_only[0] else "dve", dst3, pt[:].rearrange("p (k n) -> p k n", k=8), [pn], [dst_res])

    early = {"ids": [], "key": None}

    def pre_early(items, key, n_early=4):
        early["ids"] = [pre_a(items[t][0]) for t in range(n_early)]
        early["key"] = key

    def pre_tiles(items, key=None):
        n = len(items); ids = []
        if key is not None and early["key"] == key:
            ids = list(early["ids"]); early["ids"] = []; early["key"] = None
        n0 = len(ids)
        for t in range(n + 2):
            if n0 <= t < n:
                ids.append(pre_a(items[t][0]))
            if 1 <= t <= n:
                pre_b(ids[t - 1])
            if t >= 2:
                pre_c(ids[t - 2], items[t - 2][1], items[t - 2][2])

    def cmul(eng, o_re, o_im, a_re, a_im, b_re, b_im, t1, t2, r, w, tn):
        TT(eng, t1, a_re, b_re, ALU.mult, r, [tn + "1"])
        TT(eng, t2, a_im, b_im, ALU.mult, r, [tn + "2"])
        TT(eng, o_re, t1, t2, ALU.subtract, [tn + "1", tn + "2"], w)
        TT(eng, t1, a_re, b_im, ALU.mult, r, [tn + "1"])
        TT(eng, t2, a_im, b_re, ALU.mult, r, [tn + "2"])
        TT(eng, o_im, t1, t2, ALU.add, [tn + "1", tn + "2"], w)

    with ExitStack() as es:
        T = lambda name, shape, dt=F32: es.enter_context(nc.sbuf_tensor(name, shape, dt))
        SW = 1408
        stg = [T("stg0", [128, SW]), T("stg1", [128, SW])]
        wkvb = T("wkvb", [128, 8, 512], BF16); memT = T("memT", [128, 8, 256], BF16)
        GC = 4
        if do_ssm:
            sm = T("sm", [128, 30, G]); Bt = T("Bt", [128, 2, G * 16]); Ct = T("Ct", [128, 2, G * 16])
            dcs = T("dcs", [128, G]); PC = T("PC", [128, 2, G, 9]); RR = T("RR", [128, 2, G, 17]); PB = T("PB", [128, 2, G, 15])
            BB = T("BB", [128, 2, G, 16]); XY = T("XY", [128, 6, G, 16])
            ETre0 = T("ETre0", [128, G // 4, 128]); ETim0 = T("ETim0", [128, G // 4, 128])
            tA = T("tA", [128, 960]); tB = T("tB", [128, 960]); tC = T("tC", [128, 512]); tE = T("tE", [128, 384]); tF = T("tF", [128, 384])
            EBst = T("EBst", [128, GC, 15, 16]); EBsw = T("EBsw", [128, GC, 8, 16]); ECst = T("ECst", [128, GC, 9, 16])
            ETre1 = EBst[:].rearrange("p g i h -> p (g i h)")[:, 0:768].rearrange("p (g m) -> p g m", m=128)
            ETim1 = tA[:, 0:768].rearrange("p (g m) -> p g m", m=128)
            sm_ = lambda i: sm[:, i, :]
            sn = lambda i: f"sm{i}"

            def tt(o, a, b, alu, eng="dve"):
                TT(eng, sm_(o), sm_(a), sm_(b), alu, [sn(a), sn(b)], [sn(o)])

            def ts(o, a, s1, s2, o0, o1=ALU.bypass):
                TS("dve", sm_(o), sm_(a), s1, s2, o0, o1, [sn(a)], [sn(o)])

            ldgrp[0] = "c2"
            ld(sm_(0), lre, "sm0"); ld(sm_(1), lim, "sm1"); ld(sm_(2), ldt, "sm2")
            ld(Bt[:, 0, :], bre, "Bt"); ld(Bt[:, 1, :], bim, "Bt")
            ld(Ct[:, 0, :], cre, "Ct"); ld(Ct[:, 1, :], cim, "Ct"); ld(dcs[:], dcol, "dcs")
            ACT(sm_(3), sm_(2), AF.Exp, ["sm2"], ["sm3"])
            tt(4, 0, 3, ALU.mult); tt(5, 1, 3, ALU.mult)
            ACT(sm_(6), sm_(5), AF.Sin, ["sm5"], ["sm6"], scale=1.0 / 8)
            ACT(sm_(7), sm_(5), AF.Sin, ["sm5"], ["sm7"], scale=1.0 / 16)
            ACT(sm_(12), sm_(4), AF.Exp, ["sm4"], ["sm12"])
            ACT(sm_(15), sm_(4), AF.Exp, ["sm4"], ["sm15"], scale=-1.0)
            ACT(sm_(24), sm_(4), AF.Exp, ["sm4"], ["sm24"], scale=-8.0)
            ACT(rho8[:], sm_(4), AF.Exp, ["sm4"], ["rho8"], scale=8.0)

        pc = [0]

        def wload(dst, src, ncol, scale_ap=None):
            s_ = pc[0] % 2; pc[0] += 1
            S.dma("sp", stg[s_][:, 0:ncol], src, writes=[f"stg{s_}"], group=f"stg{s_}")
            if scale_ap is None:
                ACT(dst, stg[s_][:, 0:ncol], AF.Copy, [f"stg{s_}"], ["wts"])
            else:
                ACT(dst, stg[s_][:, 0:ncol], AF.Copy, [f"stg{s_}", "prm"], ["wts"], scale=scale_ap)
        for k in range(8):
            for hh in range(2):
                wload(win[:, k, hh * SW:(hh + 1) * SW], w_in[k * 128:(k + 1) * 128, hh * SW:(hh + 1) * SW], SW, prm[:, k:k + 1])
        for k in range(8):
            wload(wkvb[:, k, :], w_kv[k * 128:(k + 1) * 128, :], 512, prm[:, 8 + k:9 + k])
        for k in range(8):
            wload(wout[:, k, :], w_out[k * 128:(k + 1) * 128, :], 1024)
        for k in range(3):
            wload(wglu[:, k, :], w_glu[k * 128:(k + 1) * 128, :], 384)

        if do_ssm:
            tt(9, 7, 7, ALU.mult); ts(8, 9, -2.0, 1.0, ALU.mult, ALU.add)
            for _ in range(3):
                tt(9, 8, 8, ALU.mult); tt(10, 6, 6, ALU.mult); tt(11, 8, 6, ALU.mult)
                tt(8, 9, 10, ALU.subtract); ts(6, 11, 2.0, None, ALU.mult)
            tt(13, 12, 8, ALU.mult); tt(14, 12, 6, ALU.mult)
            tt(16, 15, 8, ALU.mult); tt(25, 15, 6, ALU.mult); ts(17, 25, -1.0, None, ALU.mult)
            for tab, bre_, bim_, nd in ((PC, 13, 14, 3), (RR, 16, 17, 4)):
                tn = "PC" if tab is PC else "RR"
                op("dve", lambda e, tab=tab: e.memset(tab[:, 0, :, 0], 1.0), [], [tn])
                op("dve", lambda e, tab=tab: e.memset(tab[:, 1, :, 0], 0.0), [], [tn])
                CP("dve", tab[:, 0, :, 1], sm_(bre_), [sn(bre_)], [tn])
                CP("dve", tab[:, 1, :, 1], sm_(bim_), [sn(bim_)], [tn])
                for kk in range(nd):
                    n = 1 << kk
                    tv = lambda t_: t_[:, 0:G * n].rearrange("p (g m) -> p g m", m=n)
                    cmul("dve", tab[:, 0, :, 1 + n:1 + 2 * n], tab[:, 1, :, 1 + n:1 + 2 * n],
                         tab[:, 0, :, 1:1 + n], tab[:, 1, :, 1:1 + n],
                         tab[:, 0, :, n:n + 1].to_broadcast([128, G, n]), tab[:, 1, :, n:n + 1].to_broadcast([128, G, n]),
                         tv(tA), tv(tB), [tn], [tn], "tAB")
            tv15 = lambda t_: t_[:, 0:G * 15].rearrange("p (g m) -> p g m", m=15)
            cmul("dve", PB[:, 0], PB[:, 1], RR[:, 0, :, 0:15], RR[:, 1, :, 0:15],
                 PC[:, 0, :, 7:8].to_broadcast([128, G, 15]), PC[:, 1, :, 7:8].to_broadcast([128, G, 15]),
                 tv15(tA), tv15(tB), ["PC", "RR"], ["PB"], "tAB")
            ts(18, 13, -1.0, None, ALU.add)
            tt(9, 18, 0, ALU.mult); tt(10, 14, 1, ALU.mult); tt(20, 9, 10, ALU.add)
            tt(9, 14, 0, ALU.mult); tt(10, 18, 1, ALU.mult); tt(21, 9, 10, ALU.subtract)
            tt(9, 0, 0, ALU.mult); tt(10, 1, 1, ALU.mult); tt(19, 9, 10, ALU.add)
            op("dve", lambda e: e.reciprocal(sm_(26), sm_(19)), ["sm19"], ["sm26"])
            tt(22, 20, 26, ALU.mult); tt(23, 21, 26, ALU.mult)
            b3 = lambda ap: ap.unsqueeze(2).to_broadcast([128, G, 16])
            B3 = lambda i: Bt[:, i, :].rearrange("p (g h) -> p g h", h=16)
            C3 = lambda i: Ct[:, i, :].rearrange("p (g h) -> p g h", h=16)
            v3 = lambda t_: t_[:, 0:G * 16].rearrange("p (g h) -> p g h", h=16)
            cmul("dve", BB[:, 0], BB[:, 1], b3(sm_(22)), b3(sm_(23)), B3(0), B3(1), v3(tA), v3(tB),
                 ["sm22", "sm23", "Bt"], ["BB"], "tAB")
            mT, mB, mBn, mTn = sel[:, 0:1], sel[:, 1:2], sel[:, 2:3], sel[:, 4:5]

            def stack(dst, top, top_m, bot, bot_m, r):
                TS("dve", dst, top, top_m, None, ALU.mult, ALU.bypass, r + ["sel"], ["XY"])
                STT("dve", dst, bot, bot_m, dst, ALU.mult, ALU.add, r + ["sel", "XY"], ["XY"])
            stack(XY[:, 0], BB[:, 0], mT, BB[:, 1], mB, ["BB"])
            stack(XY[:, 1], BB[:, 1], mTn, BB[:, 0], mB, ["BB"])
            stack(XY[:, 2], BB[:, 1], mT, BB[:, 0], mB, ["BB"])
            stack(XY[:, 3], BB[:, 0], mT, BB[:, 1], mBn, ["BB"])
            stack(XY[:, 4], C3(0), mT, C3(1), mBn, ["Ct"])
            stack(XY[:, 5], C3(1), mTn, C3(0), mBn, ["Ct"])
            TT("dve", sm_(28), PC[:, 0, :, 8], sm_(24), ALU.mult, ["PC", "sm24"], ["sm28"])
            TT("dve", sm_(27), PC[:, 1, :, 8], sm_(24), ALU.mult, ["PC", "sm24"], ["sm27"])
            ts(29, 27, -1.0, None, ALU.mult)
            GH = G // 4
            for gh in range(3):
                gsl = slice(gh * GH, (gh + 1) * GH)
                en = "pool"
                ETre, ETim = (ETre0, ETim0)
                trn, tin = ("ETre0", "ETim0")
                tE_, tF_, tnm = (tE, tF, "tEF")
                CP(en, ETre[:, :, 0], sm[:, 28, gsl], ["sm28"], [trn])
                CP(en, ETim[:, :, 0], sm[:, 29, gsl], ["sm29"], [tin])
                for kk in range(7):
                    n = 1 << kk
                    tv = lambda t_: t_[:, 0:GH * n].rearrange("p (g m) -> p g m", m=n)
                    cmul(en, ETre[:, :, n:2 * n], ETim[:, :, n:2 * n], ETre[:, :, 0:n], ETim[:, :, 0:n],
                         ETre[:, :, n - 1:n].to_broadcast([128, GH, n]), ETim[:, :, n - 1:n].to_broadcast([128, GH, n]),
                         tv(tE_), tv(tF_), [trn, tin], [trn, tin], tnm)
                TS(en, ESp[:, gsl, :].rearrange("p g m -> p (g m)"), ETim[:].rearrange("p g m -> p (g m)"), sel[:, 3:4],
                   0.0, ALU.mult, ALU.add, [tin, "sel"], ["ESp"])
                CP(en, ECt[:, gsl, :].rearrange("p g m -> p (g m)"), ETre[:].rearrange("p g m -> p (g m)"), [trn], ["ECt"])
            for c in range(G // GC):
                gs = slice(GC * c, GC * c + GC)
                v4 = lambda t_, ni: t_[:, 0:GC * ni * 16].rearrange("p (g i h) -> p g i h", g=GC, i=ni)

                def prod(dst, tab, lo, ni, xa, xb_, res):
                    sh = [128, GC, ni, 16]
                    TT("dve", v4(tA, ni), tab[:, 0, gs, lo:lo + ni].unsqueeze(3).to_broadcast(sh),
                       XY[:, xa, gs, :].unsqueeze(2).to_broadcast(sh), ALU.mult, ["PB", "PC", "XY"], ["tAB1"])
                    TT("dve", v4(tB, ni), tab[:, 1, gs, lo:lo + ni].unsqueeze(3).to_broadcast(sh),
                       XY[:, xb_, gs, :].unsqueeze(2).to_broadcast(sh), ALU.mult, ["PB", "PC", "XY"], ["tAB2"])
                    TT("dve", dst, v4(tA, ni), v4(tB, ni), ALU.add, ["tAB1", "tAB2"], [res])
                prod(EBst[:], PB, 0, 15, 0, 1, "EBst")
                prod(ECst[:], PC, 0, 9, 4, 5, "ECst")
                CP("dve", Wc[:, gs, :].rearrange("p g (t h) -> p g t h", t=8), ECst[:, :, 1:9, :], ["ECst"], ["Wc"])
                W = GC * 128
                for gl in range(GC):
                    TR(pA[:, gl * 128:(gl + 1) * 128], EBst[:, gl, 0:8, :], identf[:], ["EBst", "identf"], ["pA0"])
                CP("dve", Wl[:, gs, :], pA[:, 0:W].rearrange("p (g n) -> p g n", g=GC), ["pA0"], ["Wl"])
                CP("dve", Wls[:, gs, 0:64], Wl[:, gs, 64:128], ["Wl"], ["Wls"])
                CP("dve", Wls[:, gs, 64:128], Wl[:, gs, 0:64], ["Wl"], ["Wls"])
                for gl in range(GC):
                    MM(pC[:, gl * 128:(gl + 1) * 128], EBst[:, gl, 7:15, :], ECst[:, gl, 0:8, :], True, True,
                       ["EBst", "ECst"], ["pC0"])
                m3 = lambda t_: t_[:, 0:W].rearrange("p (g n) -> p g n", g=GC)
                TT("dve", m3(tA), pC[:, 0:W].rearrange("p (g n) -> p g n", g=GC),
                   maskf[:].unsqueeze(1).to_broadcast([128, GC, 128]), ALU.mult, ["pC0", "maskf"], ["tAB1"])
                TT("dve", m3(tC), identf[:].unsqueeze(1).to_broadcast([128, GC, 128]),
                   dcs[:, gs].unsqueeze(2).to_broadcast([128, GC, 128]), ALU.mult, ["identf", "dcs"], ["tC"])
                TT("dve", MI[:, gs, :], m3(tA), m3(tC), ALU.add, ["tAB1", "tC"], ["MI"])
            for gh in range(3, 4):
                gsl = slice(gh * GH, (gh + 1) * GH)
                en = "dve"
                ETre, ETim = (ETre1, ETim1)
                trn, tin = ("EBst", "tAB1")
                tE_, tF_, tnm = (tB[:, 0:384], tB[:, 384:768], "tAB2_")
                CP(en, ETre[:, :, 0], sm[:, 28, gsl], ["sm28"], [trn])
                CP(en, ETim[:, :, 0], sm[:, 29, gsl], ["sm29"], [tin])
                for kk in range(7):
                    n = 1 << kk
                    tv = lambda t_: t_[:, 0:GH * n].rearrange("p (g m) -> p g m", m=n)
                    cmul(en, ETre[:, :, n:2 * n], ETim[:, :, n:2 * n], ETre[:, :, 0:n], ETim[:, :, 0:n],
                         ETre[:, :, n - 1:n].to_broadcast([128, GH, n]), ETim[:, :, n - 1:n].to_broadcast([128, GH, n]),
                         tv(tE_), tv(tF_), [trn, tin], [trn, tin], tnm)
                TS(en, ESp[:, gsl, :].rearrange("p g m -> p (g m)"), ETim[:].rearrange("p g m -> p (g m)"), sel[:, 3:4],
                   0.0, ALU.mult, ALU.add, [tin, "sel"], ["ESp"])
                CP(en, ECt[:, gsl, :].rearrange("p g m -> p (g m)"), ETre[:].rearrange("p g m -> p (g m)"), [trn], ["ECt"])
        act_only[0] = True
        pre_tiles([(mem[mc * 128:(mc + 1) * 128, :], memT[:, :, mc * 128:(mc + 1) * 128], "memT") for mc in range(2)])
        act_only[0] = False
        for fc in range(2):
            for k in range(8):
                MM(pA[:, 512:768], wkvb[:, k, fc * 128:(fc + 1) * 128], memT[:, k, :], k == 0, k == 7,
                   ["wts", "memT"], ["pA1"])
            CP("act", kT[:, fc, :], pA[:, 512:768], ["pA1"], ["kT"])
        for mc in range(2):
            for k in range(8):
                MM(pB[:, 512:768], memT[:, k, mc * 128:(mc + 1) * 128], wkvb[:, k, 256:512], k == 0, k == 7,
                   ["wts", "memT"], ["pB1"])
            CP("act", vv[:, mc, :], pB[:, 512:768], ["pB1"], ["vv"])
        if do_ssm and stage > 1:
            pre_early([(xp[t * 128:(t + 1) * 128, :],) for t in range(4)], "p0")
        S.barrier()

    if stage <= 1:
        S.dma("sp", out[0:128, 0:256], gfs[:, 0:256], reads=["gfs"], writes=[], group="st")
        S.final_wait("st"); S.finish(); return nc
    hnT = A("hnT", [128, 8, 2 + 1024], BF16)
    Asb = A("Asb", [128, 1024], BF16); U8 = A("U8", [128, G, 128], BF16)
    b1 = A("b1", [128, 1024], BF16); b2 = A("b2", [128, 1024], BF16)
    rbf = A("rbf", [128, 1024], BF16); Sbuf = A("Sbuf", [128, G, 129], BF16)
    gT = A("gT", [128, 3, 1024], BF16)
    rend = A("rend", [128, 8], BF16); rsw = A("rsw", [128, 8], BF16); t3e = A("t3e", [128, 8], F32); t4e = A("t4e", [128, 8], F32)
    sa = A("sa", [128, 3, 512], BF16); sc = A("sc", [128, 3, 512], BF16); cbs = A("cbs", [128, 3, 512], BF16)
    ci3 = A("ci", [128, 3, 512], BF16); ub = A("ub", [128, 3, 514], BF16)
    qT = A("qT", [128, 2, 512], BF16); sq = A("sq", [128, 2, 512], BF16); pp4 = A("pp", [128, 4, 512], BF16); pp = pp4
    yT = A("yT", [128, 8, 512], BF16); rden = A("rden", [128, 512], F32); tmpb = A("tmpb", [128, 512], BF16)
    op("dve", lambda e: e.memset(Sbuf[:, :, 128], 0.0), [], ["Sb00", "Sb01", "Sb10", "Sb11", "Sb20", "Sb21"])
    if not do_ssm:
        op("dve", lambda e: e.memset(gT[:].rearrange("p c n -> p (c n)"), 0.0), [], ["gT"])

    HN = [f"hn{t}" for t in range(8)]

    def blk_items(src, base):
        return [(src[base + t * 128: base + (t + 1) * 128, :], hnT[:, :, 2 + t * 128: 2 + (t + 1) * 128], HN[t])
                for t in range(8)]

    def pre_block(src, base, key=None):
        pre_tiles(blk_items(src, base), key)

    def ssm_block(with_out, fill=None, hook=None):
        def emit_fill(n):
            if fill is None:
                return
            for _ in range(n):
                u_ = next(fill, None)
                if u_ is None:
                    return
                u_()

        def F1(c):
            gs = slice(8 * c, 8 * c + 8)
            for j8 in range(8):
                for k in range(8):
                    MM(pA[:, j8 * 128:(j8 + 1) * 128], hnT[:, k, 2 + j8: 2 + 1024: 8], win[:, k, c * 128:(c + 1) * 128],
                       k == 0, k == 7, HN + ["wts"], ["pA0", "pA1"])
            CP("act", Asb[:].rearrange("p (g j h) -> p j g h", g=8, j=8),
               pA[:].rearrange("p (j g h) -> p j g h", j=8, g=8), ["pA0", "pA1"], ["Asb"])
            for gl in range(8):
                TR(pT0[:, gl * 128:(gl + 1) * 128], Asb[:, gl * 128:(gl + 1) * 128], identb[:], ["Asb", "identb"], ["pT0"])
            CP("act", U8[:, gs, :], pT0[:].rearrange("p (g n) -> p g n", g=8), ["pT0"], [f"U8_{c}"])

        def F2(c, h):
            for gl in range(4):
                g = 8 * c + 4 * h + gl
                MM(pB[:, gl * 128:(gl + 1) * 128], Wl[:, g, :], U8[:, g, :], True, True, ["Wl", f"U8_{c}"], ["pB0"])
            for gl in range(4):
                g = 8 * c + 4 * h + gl
                MM(pB[:, 512 + gl * 128:512 + (gl + 1) * 128], Wls[:, g, :], U8[:, g, :], True, True, ["Wls", f"U8_{c}"], ["pB1"])

        def BC(c, h):
            g0 = 8 * c + 4 * h
            gs = slice(g0, g0 + 4)
            fs = slice(h * 512, (h + 1) * 512)
            b1n, b2n, rn = f"b1{h}", f"b2{h}", f"rbf{h}"
            ec = ECt[:, gs, :].rearrange("p g m -> p (g m)"); esp = ESp[:, gs, :].rearrange("p g m -> p (g m)")
            TT("dve", b1[:, fs], pB[:, 0:512], ec, ALU.mult, ["pB0", "ECt"], [b1n])
            TT("dve", b2[:, fs], pB[:, 512:1024], esp, ALU.mult, ["pB1", "ESp"], [b2n])
            TT("dve", b1[:, fs], b1[:, fs], b2[:, fs], ALU.add, [b1n, b2n], [b1n])
            for gl in range(4):
                g = g0 + gl
                cs = slice(h * 512 + gl * 128, h * 512 + (gl + 1) * 128)
                op("dve", lambda e, cs=cs, g=g: e.tensor_tensor_scan(
                    rbf[:, cs], rho8[:, g:g + 1].to_broadcast([128, 128]), b1[:, cs], send[:, g:g + 1], ALU.mult, ALU.add),
                   [b1n, "rho8", f"send{c}{h}"], [rn])
            MM(pB[:, 0:512], permb[:], rbf[:, fs], True, True, ["permb", rn], ["pB0"])
            TT("dve", b1[:, fs], rbf[:, fs], ec, ALU.mult, [rn, "ECt"], [b1n])
            TT("dve", b2[:, fs], pB[:, 0:512], esp, ALU.mult, ["pB0", "ESp"], [b2n])
            CP("pool", Sbuf[:, gs, 0:1], Sbuf[:, gs, 128:129], [f"Sb{c}{h}"], [f"Sb{c}{h}"])
            TT("dve", Sbuf[:, gs, 1:129], b1[:, fs].rearrange("p (g m) -> p g m", g=4),
               b2[:, fs].rearrange("p (g m) -> p g m", g=4), ALU.subtract, [b1n, b2n, f"Sb{c}{h}"], [f"Sb{c}{h}"])
            TT("pool", send[:, gs], b1[:, h * 512 + 127:(h + 1) * 512:128], b2[:, h * 512 + 127:(h + 1) * 512:128],
               ALU.subtract, [b1n, b2n], [f"send{c}{h}"])

        def F2full(c):
            for gl in range(8):
                g = 8 * c + gl
                MM(pB[:, gl * 128:(gl + 1) * 128], Wl[:, g, :], U8[:, g, :], True, True, ["Wl", f"U8_{c}"], ["pB0", "pB1"])
            for gl in range(8):
                g = 8 * c + gl
                MM(pC[:, gl * 128:(gl + 1) * 128], Wls[:, g, :], U8[:, g, :], True, True, ["Wls", f"U8_{c}"], ["pC0", "pC1"])

        def BCfull(c):
            gs = slice(8 * c, 8 * c + 8)
            sbn = [f"Sb{c}0", f"Sb{c}1"]; sdn = [f"send{c}0", f"send{c}1"]
            ec = ECt[:, gs, :].rearrange("p g m -> p (g m)"); esp = ESp[:, gs, :].rearrange("p g m -> p (g m)")
            TT("dve", b1[:], pB[:], ec, ALU.mult, ["pB0", "pB1", "ECt"], ["b10", "b11"])
            TT("dve", b2[:], pC[:], esp, ALU.mult, ["pC0", "pC1", "ESp"], ["b20", "b21"])
            TT("dve", b1[:], b1[:], b2[:], ALU.add, ["b10", "b11", "b20", "b21"], ["b10", "b11"])
            for gl in range(8):
                g = 8 * c + gl
                op("dve", lambda e, gl=gl, g=g: e.tensor_tensor_scan(
                    rbf[:, gl * 128:(gl + 1) * 128], rho8[:, g:g + 1].to_broadcast([128, 128]),
                    b1[:, gl * 128:(gl + 1) * 128], send[:, g:g + 1], ALU.mult, ALU.add),
                   ["b10", "b11", "rho8"] + sdn, ["rbf0", "rbf1"])
            CP("pool", rend[:], rbf[:, 127:1024:128], ["rbf0", "rbf1"], ["rend"])
            S.dma("sp", rsw[64:128, :], rend[0:64, :], reads=["rend"], writes=["rswA"], group="rswA")
            S.dma("sp", rsw[0:64, :], rend[64:128, :], reads=["rend"], writes=["rswB"], group="rswB")
            TT("pool", t3e[:], rend[:], ECt[:, gs, 127], ALU.mult, ["rend", "ECt"], ["t3e"])
            TT("pool", t4e[:], rsw[:], ESp[:, gs, 127], ALU.mult, ["rswA", "rswB", "ESp"], ["t4e"])
            TT("pool", send[:, gs], t3e[:], t4e[:], ALU.subtract, ["t3e", "t4e"], sdn)
            CP("pool", Sbuf[:, gs, 128], send[:, gs], sdn + sbn, sbn)

        def Dst(c):
            for gl in range(8):
                g = 8 * c + gl
                MM(pA[:, gl * 128:(gl + 1) * 128], U8[:, g, :], MI[:, g, :], True, False, [f"U8_{c}", "MI"], ["pA0", "pA1"])
                MM(pA[:, gl * 128:(gl + 1) * 128], Sbuf[:, g, 0:128], Wc[:, g, :], False, True,
                   [f"Sb{c}0", f"Sb{c}1", "Wc"], ["pA0", "pA1"])
            ACT(Asb[:].rearrange("p (t g h) -> p g t h", t=8, g=8),
                pA[:].rearrange("p (g t h) -> p g t h", g=8, t=8), AF.Gelu_apprx_tanh, ["pA0", "pA1"], ["Asb"])
            for t8 in range(8):
                TR(pT1[:, t8 * 128:(t8 + 1) * 128], Asb[:, t8 * 128:(t8 + 1) * 128], identb[:], ["Asb", "identb"], ["pT1"])
            CP("act", gT[:, c, :], pT1[:], ["pT1"], ["gT"])

        if fill is None and not with_out:
            F1(0); F2full(0); F1(1); BCfull(0); F2full(1); F1(2)
            if hook is not None:
                hook()
            BCfull(1); F2full(2); BCfull(2)
            return
        F1(0); F2(0, 0); emit_fill(3 if with_out else 0)
        for c in range(3):
            BC(c, 0); F2(c, 1); emit_fill(3 if c == 0 else 2)
            if c + 1 < 3:
                F1(c + 1); emit_fill(2 if c == 0 else 1)
            BC(c, 1)
            if c + 1 < 3:
                F2(c + 1, 0); emit_fill(2)
            if with_out:
                Dst(c); emit_fill(1)
        emit_fill(1000)

    bank_sets = {"all": [(pA, 0, "pA0"), (pA, 1, "pA1"), (pB, 0, "pB0"), (pB, 1, "pB1")],
                 "pc": [(pC, 0, "pC0"), (pC, 1, "pC1")]}
    pwc = [0]

    def mixer_units(blk, first, bs):
        banks = bank_sets[bs]
        c0 = 2 + blk * 512
        cols = slice(c0, c0 + 512)
        hres = HN[4 * blk:4 * blk + 4]

        def nextbank():
            t_, h_, nm_ = banks[pwc[0] % len(banks)]; pwc[0] += 1
            return t_[:, h_ * 512:(h_ + 1) * 512], nm_

        def projb(fi):
            bank, nm_ = nextbank()
            for k in range(8):
                MM(bank, win[:, k, fi * 128:(fi + 1) * 128], hnT[:, k, cols], k == 0, k == 7, hres + ["wts"], [nm_])
            return bank, nm_
        units = []
        if first:
            units.append(lambda: pre_tiles([(xh, tmp_halo, "pp0")]))

        def silu_unit(fi, dst, res):
            def u_():
                b_, n_ = projb(fi)
                ACT(dst, b_, AF.Silu, [n_], [res])
            return u_
        for i in range(3):
            units.append(silu_unit(3 + i, sa[:, i, :], "sa"))
        for i in range(3):
            units.append(silu_unit(15 + i, sc[:, i, :], "sc"))
        for i in range(2):
            units.append(silu_unit(20 + i, sq[:, i, :], "sq"))

        def conv_in(i):
            ci = ci3[:, i, :]
            cin = f"ci{i}"

            def u1():
                if first:
                    bh, nh = nextbank()
                    for k in range(8):
                        MM(bh[:, 0:2], win[:, k, (6 + i) * 128:(7 + i) * 128], tmp_halo[:, k, 0:2], k == 0, k == 7,
                           ["pp0", "wts"], [nh])
                    CP("act", ci[:, 0:2], bh[:, 0:2], [nh], [cin])
                    bh2, nh2 = nextbank()
                    for k in range(8):
                        MM(bh2[:, 0:2], win[:, k, (12 + i) * 128:(13 + i) * 128], tmp_halo[:, k, 0:2], k == 0, k == 7,
                           ["pp0", "wts"], [nh2])
                    TT("dve", ub[:, i, 0:2], bh2[:, 0:2], ci[:, 0:2], ALU.mult, [nh2, cin], [f"ub{i}"])
                else:
                    CP("pool", ub[:, i, 0:2], ub[:, i, 512:514], [f"ub{i}"], [f"ub{i}"])
                b_, n_ = projb(6 + i)
                CP("act", ci, b_, [n_], [cin])

            def u2():
                b_, n_ = projb(12 + i)
                CP("act", ub[:, i, 2:514], b_, [n_], [f"ub{i}"])

            def u2b():
                TT("dve", ub[:, i, 2:514], ub[:, i, 2:514], ci, ALU.mult, [f"ub{i}", cin], [f"ub{i}"])

            def u3():
                b_, n_ = projb(9 + i)
                CP("act", cbs[:, i, :], b_, [n_], ["cbs"])
            return [u1, u2, u3, u2b]
        late = []
        for i in range(3):
            u1_, u2_, u3_, u2b_ = conv_in(i)
            units += [u1_, u2_, u3_]
            late.append(u2b_)

        def q_unit(i):
            def u_():
                b_, n_ = projb(18 + i)
                CP("act", qT[:, i, :], b_, [n_], ["qT"])
            return u_
        for i in range(2):
            units.append(q_unit(i))

        def conv_unit(i):
            def u_():
                bank, nm_ = nextbank()
                for tap in range(3):
                    MM(bank, cdiag[:, i * 3 + tap, :], ub[:, i, tap:tap + 512], tap == 0, tap == 2, ["cdiag", f"ub{i}"], [nm_])
                STT("dve", tmpb[:], bank, prm[:, 22 + i:23 + i], cbs[:, i, :], ALU.add, ALU.mult,
                    [nm_, "prm", "cbs"], ["tmpb"])
                TT("dve", yT[:, 3 + i, :], tmpb[:], sc[:, i, :], ALU.mult, ["tmpb", "sc"], [f"yT{3 + i}"])
            return u_
        units += late
        for i in range(3):
            units.append(conv_unit(i))
        return units

    def mixer_rest(hf, blk):
        prefetch_x(hf, blk)
        gview = lambda c: gT[:, c, :].rearrange("p (j n) -> p j n", j=8)[:, :, blk * 64:(blk + 1) * 64]
        tmp2 = rden[:].bitcast(BF16)[:, 0:512]
        for oc in range(3):
            tb, tbn = (tmpb[:], "tmpb") if oc % 2 == 0 else (tmp2, "rden")
            t3v = tb.rearrange("p (j n) -> p j n", j=8)
            for c in range(3):
                MM(pC[:, (oc % 2) * 512:(oc % 2 + 1) * 512], wglu[:, c, oc * 128:(oc + 1) * 128], gview(c), c == 0, c == 2,
                   ["wts", "gT"], [f"pC{oc % 2}"])
            ACT(tb, pC[:, (oc % 2) * 512:(oc % 2 + 1) * 512], AF.Tanh, [f"pC{oc % 2}", "prm"], [tbn], scale=0.5,
                bias=prm[:, 19 + oc:20 + oc])
            STT("dve", t3v, t3v, 1.0, gview(oc), ALU.add, ALU.mult, [tbn, "gT"], [tbn])
            STT("dve", yT[:, oc, :].rearrange("p (n j) -> p n j", j=8), tb.rearrange("p (j n) -> p n j", j=8), 0.5,
                sa[:, oc, :].rearrange("p (n j) -> p n j", j=8), ALU.mult, ALU.mult, [tbn, "sa"], [f"yT{oc}"])
        sbank = {0: (pA, "pA"), 1: (pC, "pC")}

        def scores(h):
            pr, hh = divmod(h, 2)
            rows = slice(hh * 64, hh * 64 + 64)
            bt, bn = sbank[hh]
            for mc in range(2):
                MM(bt[:, mc * 512:(mc + 1) * 512], kT[rows, pr, mc * 128:(mc + 1) * 128], qT[rows, pr, :], True, True,
                   ["kT", "qT"], [f"{bn}{mc}"])

        def expo(h):
            hh = h % 2
            bt, bn = sbank[hh]
            for mc in range(2):
                ACT(pp4[:, 2 * hh + mc, :], bt[:, mc * 512:(mc + 1) * 512], AF.Exp, [f"{bn}{mc}"], [f"pp{hh}"], scale=0.125)

        def pv(h):
            pr, hh = divmod(h, 2)
            rows = slice(hh * 64, hh * 64 + 64)
            for mc in range(2):
                MM(pB[rows, 0:512], vv[:, mc, h * 64:(h + 1) * 64], pp4[:, 2 * hh + mc, :], mc == 0, mc == 1,
                   ["vv", f"pp{hh}"], ["pB0"])
            for mc in range(2):
                MM(pB[rows, 512:1024], onesb[:], pp4[:, 2 * hh + mc, :], mc == 0, mc == 1, ["onesb", f"pp{hh}"], ["pB1"])

        def fin(pr):
            ACT(rden[:], pB[:, 512:1024], AF.Ln, ["pB1"], ["rden"])
            ACT(rden[:], rden[:], AF.Exp, ["rden"], ["rden"], scale=-1.0)
            TT("dve", tmpb[:], pB[:, 0:512], rden[:], ALU.mult, ["pB0", "rden"], ["tmpb"])
            TT("dve", yT[:, 6 + pr, :], tmpb[:], sq[:, pr, :], ALU.mult, ["tmpb", "sq"], [f"yT{6 + pr}"])
        scores(0); expo(0); scores(1); expo(1); pv(0); scores(2); pv(1); fin(0)
        expo(2); scores(3); expo(3); pv(2); pv(3); fin(1)
        tiles = []
        for j in range(4):
            i = tilectr[0]; tilectr[0] += 1
            tiles.append((i % NXB, hf * 1024 + blk * 512 + j * 128))
        for j, (sl, tok) in enumerate(tiles):
            xb = xbuf[sl]; xn = f"xb{sl}"
            pO, pOn = (pC, "pC") if j % 2 == 0 else (pB, "pB")
            for nh in range(2):
                for kc in range(8):
                    MM(pO[:, nh * 512:(nh + 1) * 512], yT[:, kc, j * 128:(j + 1) * 128], wout[:, kc, nh * 512:(nh + 1) * 512],
                       kc == 0, kc == 7, [f"yT{kc}", "wts"], [f"{pOn}{nh}"])
            TT("dve", xb[:], pO[:], xb[:], ALU.add, [f"{pOn}0", f"{pOn}1", xn], [xn])
            ACT(junkD, xb[:], AF.Square, [xn], ["junk1", f"ssqB{j}"], accum_out=stat2[:, j:j + 1])
            ACT(stat2[:, 4 + j:5 + j], stat2[:, j:j + 1], AF.Ln, [f"ssqB{j}", "epsc"], [f"vB{j}"], scale=1.0 / D, bias=epsc[:, 0:1])
            ACT(stat2[:, 8 + j:9 + j], stat2[:, 4 + j:5 + j], AF.Exp, [f"vB{j}"], [f"rsB{j}"], scale=-0.5)
            STT("dve", xb[:], xb[:], stat2[:, 8 + j:9 + j], gfs[:], ALU.mult, ALU.mult, [xn, f"rsB{j}", "gfs"], [xn])
            S.dma("sp", out[tok:tok + 128, :], xb[:], reads=[xn], writes=[], group=f"st{sl}")

    def prefetch_x(hf, blk):
        for j in range(4):
            sl = (tilectr[0] + j) % NXB
            tok = hf * 1024 + blk * 512 + j * 128
            S.dma("sp", xbuf[sl][:], xs[tok:tok + 128, :], writes=[f"xb{sl}"], group=f"xb{sl}")

    tmp_halo = pp4[:, 0:2, :].rearrange("p a (b n) -> p (a b) n", b=4)

    if do_ssm:
        pre_block(xp, 0, "p0")
        ssm_block(False, hook=lambda: pre_early(blk_items(xp, 1024), "p1"))
        pre_block(xp, 1024, "p1")
        ssm_block(False, hook=lambda: pre_early(blk_items(xs, 0), "o0"))
    if stage == 2:
        S.dma("sp", out[0:128, 0:256], gfs[:, 0:256], reads=["Sbuf", "send", "gfs"], writes=[], group="st")
        S.final_wait("st"); S.finish(); return nc
    for hf in range(2):
        pre_block(xs, hf * 1024, "o0" if (hf == 0 and do_ssm) else None)
        u0 = mixer_units(0, hf == 0, "pc" if do_ssm else "all")
        if do_ssm:
            ssm_block(True, iter(u0))
        else:
            for u_ in u0:
                u_()
        mixer_rest(hf, 0)
        for u_ in mixer_units(1, False, "all"):
            u_()
        mixer_rest(hf, 1)
    for i_ in range(NXB):
        S.final_wait(f"st{i_}")
    S.finish()
    return nc


def kernel(**inputs):
    f = lambda a: np.ascontiguousarray(np.asarray(a, dtype=np.float32))
    x = f(inputs["x"]); mem = f(inputs["mem"])
    t128 = lambda a: np.ascontiguousarray(np.tile(a, (2,) + (1,) * (a.ndim - 1)))
    lam_re = f(inputs["ssm_lam_re"])[0]; lam_im = f(inputs["ssm_lam_im"])[0]
    shared = {
        "w_in": f(inputs["w_in"])[0], "w_out": f(inputs["w_out"])[0],
        "w_glu": f(inputs["w_glu"])[0], "w_kv": f(inputs["w_mem_kv"])[0],
        "gin": f(f(inputs["norm_g"])[0].reshape(8, 128).T),
        "gmem": f(f(inputs["mem_norm_g"])[0].reshape(8, 128).T),
        "gfin": f(np.tile(f(inputs["final_norm_g"]).reshape(1, D), (128, 1))),
        "bglu": f(f(inputs["b_glu"])[0].reshape(3, 128).T),
        "cw": f(f(inputs["conv_w"])[0].reshape(3, 3, 128).transpose(2, 1, 0).reshape(128, 9)),
        "cbias": f(f(inputs["conv_b"])[0].reshape(3, 128).T),
        "lre": t128(f(lam_re.T)), "lim": t128(f(lam_im.T)),
        "ldt": f(np.tile(f(inputs["ssm_log_dt"])[0].reshape(1, G), (128, 1))),
        "bre": t128(f(f(inputs["ssm_b_re"])[0].transpose(1, 0, 2).reshape(64, G * 16))),
        "bim": t128(f(f(inputs["ssm_b_im"])[0].transpose(1, 0, 2).reshape(64, G * 16))),
        "cre": t128(f(f(inputs["ssm_c_re"])[0].transpose(2, 0, 1).reshape(64, G * 16))),
        "cim": t128(f(f(inputs["ssm_c_im"])[0].transpose(2, 0, 1).reshape(64, G * 16))),
        "dcol": f(np.tile(f(inputs["ssm_d"])[0].reshape(G, 16).T, (8, 1))),
        "ident": np.eye(128, dtype=np.float32),
        "mask": f(np.kron(np.triu(np.ones((8, 8), np.float32)), np.ones((16, 16), np.float32))),
        "perm": f(np.roll(np.eye(128, dtype=np.float32), 64, axis=1)),
        "sel": f(np.stack([np.r_[np.ones(64), np.zeros(64)], np.r_[np.zeros(64), np.ones(64)],
                           np.r_[np.zeros(64), -np.ones(64)], np.r_[-np.ones(64), np.ones(64)],
                           np.r_[-np.ones(64), np.zeros(64)], np.r_[np.ones(64), np.ones(64)]], axis=1)),
    }
    in_maps = []
    for b in range(4):
        for s in range(2):
            m = dict(shared)
            m["xs"] = f(x[b, s * NT:(s + 1) * NT])
            m["xp"] = f(x[b, 0:NT]) if s == 1 else np.zeros((NT, D), np.float32)
            xh = np.zeros((128, D), np.float32)
            if s == 1:
                xh[0:2] = x[b, NT - 2:NT]
            m["xh"] = xh
            m["mem"] = f(mem[b])
            in_maps.append(m)
    nc = build_nc(stage=STAGE)
    res = run_bass_kernel_spmd(nc, in_maps, core_ids=list(range(8)))
    outp = np.empty((4, 2 * NT, D), np.float32)
    for b in range(4):
        for s in range(2):
            outp[b, s * NT:(s + 1) * NT] = res.results[2 * b + s]["out"]
    return outp
```

```python
from contextlib import ExitStack

import numpy as np
import concourse.bass as bass
import concourse.mybir as mybir
from concourse.bass_utils import run_bass_kernel_spmd

F32 = mybir.dt.float32
BF16 = mybir.dt.bfloat16
AF = mybir.ActivationFunctionType
ALU = mybir.AluOpType


class _Op:
    __slots__ = ("eng", "fn", "deps", "signaled", "seq", "dma_group", "dma_val", "idx")

    def __init__(self, eng, fn):
        self.eng = eng
        self.fn = fn
        self.deps = []
        self.signaled = False
        self.seq = 0
        self.dma_group = None
        self.dma_val = 0
        self.idx = 0


class Sched:
    ENG = ("pe", "act", "dve", "pool", "sp")

    def __init__(self, nc):
        self.nc = nc
        self.ops = []
        self.byeng = {e: [] for e in self.ENG}
        self.res = {}
        self.groups = {}
        self.final = []
        self.pending = {}

    def _track(self, op, reads, writes):
        deps = {}

        def add(o, kind):
            if o is None or o is op:
                return
            if o.dma_group is not None and o.dma_group == op.dma_group and self.groups[o.dma_group]["bulk"]:
                return
            if o.eng == op.eng and o.dma_group is None:
                if op.eng == "pe":
                    return
            deps[o.idx] = o

        for r in reads:
            st = self.res.get(r)
            if st is not None:
                add(st[0], "raw")
        for w in writes:
            st = self.res.get(w)
            if st is not None:
                add(st[0], "waw")
                for rd in st[1]:
                    add(rd, "war")
        for r in reads:
            st = self.res.setdefault(r, [None, []])
            st[1].append(op)
        for w in writes:
            self.res[w] = [op, []]
        op.deps = list(deps.values())
        for o in op.deps:
            o.signaled = True

    def op(self, eng, fn, reads=(), writes=(), _grp=None):
        o = _Op(eng, fn)
        o.dma_group = _grp
        o.idx = len(self.ops)
        self.ops.append(o)
        self.byeng[eng].append(o)
        self._track(o, reads, writes)
        pend = self.pending.pop(eng, None)
        if pend:
            have = {d.idx for d in o.deps}
            for d in pend:
                if d.idx not in have and not (d.eng == eng and d.dma_group is None):
                    o.deps.append(d)
                    d.signaled = True
        return o

    def barrier(self):
        last = []
        for e in ("pe", "act", "dve", "pool"):
            ops = [o for o in self.byeng[e] if o.dma_group is None]
            if ops:
                last.append(ops[-1])
        seen = {}
        for o in self.ops:
            if o.dma_group is not None:
                seen[o.dma_group] = o
        last += list(seen.values())
        self.pending = {e: list(last) for e in self.ENG}

    def dma(self, eng, out, in_, reads=(), writes=(), group="misc", bulk=False, **kw):
        g = self.groups.setdefault(group, {"n": 0, "bulk": bulk})
        g["n"] += 1
        o = self.op(eng, lambda e: e.dma_start(out=out, in_=in_, **kw), reads, writes, _grp=group)
        o.dma_val = 16 * g["n"]
        return o

    def final_wait(self, group):
        self.final.append(group)

    def finish(self):
        nc = self.nc
        with ExitStack() as st:
            sems = {}
            for e in ("pe", "act", "dve", "pool"):
                sems[e] = st.enter_context(nc.semaphore("s_" + e))
            for gname in self.groups:
                sems["g_" + gname] = st.enter_context(nc.semaphore("g_" + gname))
            for e in ("pe", "act", "dve", "pool"):
                c = 0
                for o in self.byeng[e]:
                    if o.dma_group is None and o.signaled:
                        c += 1
                        o.seq = c

            def token(o):
                if o.dma_group is not None:
                    g = self.groups[o.dma_group]
                    v = 16 * g["n"] if g["bulk"] else o.dma_val
                    return "g_" + o.dma_group, v
                return o.eng, o.seq

            block = st.enter_context(nc.Block())

            def run(eng_name):
                def body(e):
                    waited = {}
                    for o in self.byeng[eng_name]:
                        for d in o.deps:
                            k, v = token(d)
                            if waited.get(k, 0) < v:
                                e.wait_ge(sems[k], v)
                                waited[k] = v
                        ins = o.fn(e)
                        if o.dma_group is not None:
                            ins.then_inc(sems["g_" + o.dma_group], 16)
                        elif o.signaled:
                            ins.then_inc(sems[eng_name], 1)
                    if eng_name == "sp":
                        for gname in self.final:
                            e.wait_ge(sems["g_" + gname], 16 * self.groups[gname]["n"])
                return body

            block.tensor(run("pe"))
            block.scalar(run("act"))
            block.vector(run("dve"))
            block.gpsimd(run("pool"))
            block.sync(run("sp"))


G, P = 24, 64
STAGE = 9
D, DIN, NT = 1024, 2816, 2048
EPS = 1e-6


def build_nc(do_ssm=True, stage=9):
    nc = bass.Bass("TRN2", target_bir_lowering=False)
    S = Sched(nc)

    def dram(name, shape, kind="ExternalInput"):
        return nc.dram_tensor(name, list(shape), F32, kind=kind).ap()

    xs = dram("xs", [NT, D]); xp = dram("xp", [NT, D]); xh = dram("xh", [128, D])
    mem = dram("mem", [256, D])
    w_in = dram("w_in", [D, DIN]); w_out = dram("w_out", [D, D])
    w_glu = dram("w_glu", [384, 384]); w_kv = dram("w_kv", [D, 512])
    gin = dram("gin", [128, 8]); gmem = dram("gmem", [128, 8]); gfin = dram("gfin", [128, D])
    bglu = dram("bglu", [128, 3]); cw = dram("cw", [128, 9]); cbias = dram("cbias", [128, 3])
    lre = dram("lre", [128, G]); lim = dram("lim", [128, G]); ldt = dram("ldt", [128, G])
    bre = dram("bre", [128, G * 16]); bim = dram("bim", [128, G * 16])
    cre = dram("cre", [128, G * 16]); cim = dram("cim", [128, G * 16])
    dcol = dram("dcol", [128, G])
    cident = dram("ident", [128, 128]); cmask = dram("mask", [128, 128]); cperm = dram("perm", [128, 128])
    csel = dram("sel", [128, 6])
    out = dram("out", [NT, D], kind="ExternalOutput")

    A = nc.alloc_sbuf_tensor
    win = A("win", [128, 8, DIN], BF16); wout = A("wout", [128, 8, D], BF16)
    wglu = A("wglu", [128, 3, 384], BF16)
    kT = A("kT", [128, 2, 256], BF16); vv = A("vv", [128, 2, 256], BF16)
    identf = A("identf", [128, 128], F32); identb = A("identb", [128, 128], BF16)
    maskf = A("maskf", [128, 128], F32); permf = A("permf", [128, 128], F32); permb = A("permb", [128, 128], BF16)
    onesb = A("onesb", [128, 64], BF16)
    sel = A("sel_sb", [128, 6], F32)
    gfs = A("gfs", [128, D], F32)
    prm = A("prm", [128, 32], F32)
    cws = A("cws", [128, 9], F32); cdiag = A("cdiag", [128, 9, 128], BF16)
    MI = A("MI", [128, G, 128], BF16); Wl = A("Wl", [128, G, 128], BF16)
    Wls = A("Wls", [128, G, 128], BF16); Wc = A("Wc", [128, G, 128], BF16)
    ECt = A("ECt", [128, G, 128], BF16); ESp = A("ESp", [128, G, 128], BF16)
    rho8 = A("rho8", [128, G], F32); send = A("send", [128, G], F32)
    stat = A("stat", [128, 16], F32); stat2 = A("stat2", [128, 12], F32); mhalf = A("mhalf", [128, 1], F32); epsc = A("epsc", [128, 1], F32)
    pA = nc.alloc_psum_tensor("pA", [128, 1024], F32); pB = nc.alloc_psum_tensor("pB", [128, 1024], F32)
    pC = nc.alloc_psum_tensor("pC", [128, 1024], F32)
    pT0 = nc.alloc_psum_tensor("pT0", [128, 1024], BF16); pT1 = nc.alloc_psum_tensor("pT1", [128, 1024], BF16)

    op = S.op
    TT = lambda eng, o, a, b, alu, r, w: op(eng, lambda e: e.tensor_tensor(o, a, b, alu), r, w)
    TS = lambda eng, o, a, s1, s2, o0, o1, r, w: op(eng, lambda e: e.tensor_scalar(o, a, s1, s2, o0, o1), r, w)
    STT = lambda eng, o, a, s, b, o0, o1, r, w: op(eng, lambda e: e.scalar_tensor_tensor(o, a, s, b, o0, o1), r, w)
    ACT = lambda o, a, f, r, w, **kw: op("act", lambda e: e.activation(o, a, f, **kw), r, w)
    CP = lambda eng, o, a, r, w: (op(eng, lambda e: e.tensor_copy(o, a), r, w) if eng != "act"
                                  else op(eng, lambda e: e.activation(o, a, AF.Copy), r, w))
    MM = lambda o, l, rh, st, sp, r, w: op("pe", lambda e: e.matmul(o, l, rh, start=st, stop=sp), r, w)
    TR = lambda o, i, idn, r, w: op("pe", lambda e: e.transpose(o, i, idn), r, w)

    ldgrp = ["c"]

    def ld(dst, src, name):
        S.dma("sp", dst, src, writes=[name], group=ldgrp[0], bulk=True)
    ld(identf[:], cident, "identf"); ld(maskf[:], cmask, "maskf"); ld(permf[:], cperm, "permf")
    ld(sel[:], csel, "sel"); ld(gfs[:], gfin, "gfs")
    ld(prm[:, 0:8], gin, "prm"); ld(prm[:, 8:16], gmem, "prm"); ld(prm[:, 16:19], bglu, "prm")
    ld(prm[:, 22:25], cbias, "prm"); ld(cws[:], cw, "cws")
    CP("dve", identb[:], identf[:], ["identf"], ["identb"])
    CP("dve", permb[:], permf[:], ["permf"], ["permb"])
    op("dve", lambda e: e.memset(onesb[:], 1.0), [], ["onesb"])
    op("dve", lambda e: e.memset(mhalf[:], -0.5), [], ["mhalf"])
    op("dve", lambda e: e.memset(epsc[:], EPS), [], ["epsc"])
    op("dve", lambda e: e.memset(send[:], 0.0), [], ["send00", "send01", "send10", "send11", "send20", "send21"])
    TS("dve", prm[:, 19:22], prm[:, 16:19], 0.5, None, ALU.mult, ALU.bypass, ["prm"], ["prm"])
    for i in range(9):
        TS("dve", cdiag[:, i, :], identf[:], cws[:, i:i + 1], None, ALU.mult, ALU.bypass,
           ["identf", "cws"], ["cdiag"])

    NXB = 4
    xbuf = [A(f"xb{i}", [128, D], F32) for i in range(NXB)]
    xsbs = [A(f"xsb{i}", [128, D], BF16) for i in range(2)]
    junk1 = A("junk1", [128, 2], BF16)
    junkD = junk1[:, 0:1].to_broadcast([128, D])
    tilectr = [0]

    def rstd_from(ssq_ap, v_ap, r_ap, names):
        TS("dve", v_ap, ssq_ap, 1.0 / D, EPS, ALU.mult, ALU.add, [names[0]], [names[1]])
        ACT(v_ap, v_ap, AF.Sqrt, [names[1]], [names[1]])
        op("dve", lambda e: e.reciprocal(r_ap, v_ap), [names[1]], [names[2]])

    def pre_a(src):
        i = tilectr[0]; tilectr[0] += 1
        sl = i % NXB; st_ = i % 4
        S.dma("sp", xbuf[sl][:], src, writes=[f"xb{sl}"], group=f"xb{sl}")
        ACT(junkD, xbuf[sl][:], AF.Square, [f"xb{sl}"], ["junk1", f"ssq{st_}"], accum_out=stat[:, st_:st_ + 1])
        return i

    act_only = [False]

    def pre_b(i):
        sl = i % NXB; st_ = i % 4; xn = f"xb{sl}"
        nm = [f"ssq{st_}", f"v{st_}", f"rs{st_}"]
        xs_ = xsbs[i % 2]; xsn = f"xsb{i % 2}"
        if act_only[0]:
            ACT(stat[:, 4 + st_:5 + st_], stat[:, st_:st_ + 1], AF.Ln, [nm[0], "epsc"], [nm[1]], scale=1.0 / D, bias=epsc[:, 0:1])
            ACT(stat[:, 8 + st_:9 + st_], stat[:, 4 + st_:5 + st_], AF.Exp, [nm[1]], [nm[2]], scale=-0.5)
            ACT(xs_[:], xbuf[sl][:], AF.Copy, [xn, nm[2]], [xsn], scale=stat[:, 8 + st_:9 + st_])
        else:
            ACT(stat[:, 4 + st_:5 + st_], stat[:, st_:st_ + 1], AF.Ln, [nm[0], "epsc"], [nm[1]], scale=1.0 / D, bias=epsc[:, 0:1])
            ACT(stat[:, 8 + st_:9 + st_], stat[:, 4 + st_:5 + st_], AF.Exp, [nm[1]], [nm[2]], scale=-0.5)
            TS("dve", xs_[:], xbuf[sl][:], stat[:, 8 + st_:9 + st_], None, ALU.mult, ALU.bypass, [xn, nm[2]], [xsn])
        pt = pT0 if i % 2 == 0 else pT1; pn = "pT0" if i % 2 == 0 else "pT1"
        for k in range(8):
            TR(pt[:, k * 128:(k + 1) * 128], xs_[:, k * 128:(k + 1) * 128], identb[:], [xsn, "identb"], [pn])

    def pre_c(i, dst3, dst_res):
        pt = pT0 if i % 2 == 0 else pT1; pn = "pT0" if i % 2 == 0 else "pT1"
        CP("act" if act_only[0] else "dve", dst3, pt[:].rearrange("p (k n) -> p k n", k=8), [pn], [dst_res])

    early = {"ids": [], "key": None}

    def pre_early(items, key, n_early=4):
        early["ids"] = [pre_a(items[t][0]) for t in range(n_early)]
        early["key"] = key

    def pre_tiles(items, key=None):
        n = len(items); ids = []
        if key is not None and early["key"] == key:
            ids = list(early["ids"]); early["ids"] = []; early["key"] = None
        n0 = len(ids)
        for t in range(n + 2):
            if n0 <= t < n:
                ids.append(pre_a(items[t][0]))
            if 1 <= t <= n:
                pre_b(ids[t - 1])
            if t >= 2:
                pre_c(ids[t - 2], items[t - 2][1], items[t - 2][2])

    def cmul(eng, o_re, o_im, a_re, a_im, b_re, b_im, t1, t2, r, w, tn):
        TT(eng, t1, a_re, b_re, ALU.mult, r, [tn + "1"])
        TT(eng, t2, a_im, b_im, ALU.mult, r, [tn + "2"])
        TT(eng, o_re, t1, t2, ALU.subtract, [tn + "1", tn + "2"], w)
        TT(eng, t1, a_re, b_im, ALU.mult, r, [tn + "1"])
        TT(eng, t2, a_im, b_re, ALU.mult, r, [tn + "2"])
        TT(eng, o_im, t1, t2, ALU.add, [tn + "1", tn + "2"], w)

    with ExitStack() as es:
        T = lambda name, shape, dt=F32: es.enter_context(nc.sbuf_tensor(name, shape, dt))
        SW = 1408
        stg = [T("stg0", [128, SW]), T("stg1", [128, SW])]
        wkvb = T("wkvb", [128, 8, 512], BF16); memT = T("memT", [128, 8, 256], BF16)
        GC = 4
        if do_ssm:
            sm = T("sm", [128, 30, G]); Bt = T("Bt", [128, 2, G * 16]); Ct = T("Ct", [128, 2, G * 16])
            dcs = T("dcs", [128, G]); PC = T("PC", [128, 2, G, 9]); RR = T("RR", [128, 2, G, 17]); PB = T("PB", [128, 2, G, 15])
            BB = T("BB", [128, 2, G, 16]); XY = T("XY", [128, 6, G, 16])
            ETre0 = T("ETre0", [128, G // 4, 128]); ETim0 = T("ETim0", [128, G // 4, 128])
            tA = T("tA", [128, 960]); tB = T("tB", [128, 960]); tC = T("tC", [128, 512]); tE = T("tE", [128, 384]); tF = T("tF", [128, 384])
            EBst = T("EBst", [128, GC, 15, 16]); EBsw = T("EBsw", [128, GC, 8, 16]); ECst = T("ECst", [128, GC, 9, 16])
            ETre1 = EBst[:].rearrange("p g i h -> p (g i h)")[:, 0:768].rearrange("p (g m) -> p g m", m=128)
            ETim1 = tA[:, 0:768].rearrange("p (g m) -> p g m", m=128)
            sm_ = lambda i: sm[:, i, :]
            sn = lambda i: f"sm{i}"

            def tt(o, a, b, alu, eng="dve"):
                TT(eng, sm_(o), sm_(a), sm_(b), alu, [sn(a), sn(b)], [sn(o)])

            def ts(o, a, s1, s2, o0, o1=ALU.bypass):
                TS("dve", sm_(o), sm_(a), s1, s2, o0, o1, [sn(a)], [sn(o)])

            ldgrp[0] = "c2"
            ld(sm_(0), lre, "sm0"); ld(sm_(1), lim, "sm1"); ld(sm_(2), ldt, "sm2")
            ld(Bt[:, 0, :], bre, "Bt"); ld(Bt[:, 1, :], bim, "Bt")
            ld(Ct[:, 0, :], cre, "Ct"); ld(Ct[:, 1, :], cim, "Ct"); ld(dcs[:], dcol, "dcs")
            ACT(sm_(3), sm_(2), AF.Exp, ["sm2"], ["sm3"])
            tt(4, 0, 3, ALU.mult); tt(5, 1, 3, ALU.mult)
            ACT(sm_(6), sm_(5), AF.Sin, ["sm5"], ["sm6"], scale=1.0 / 8)
            ACT(sm_(7), sm_(5), AF.Sin, ["sm5"], ["sm7"], scale=1.0 / 16)
            ACT(sm_(12), sm_(4), AF.Exp, ["sm4"], ["sm12"])
            ACT(sm_(15), sm_(4), AF.Exp, ["sm4"], ["sm15"], scale=-1.0)
            ACT(sm_(24), sm_(4), AF.Exp, ["sm4"], ["sm24"], scale=-8.0)
            ACT(rho8[:], sm_(4), AF.Exp, ["sm4"], ["rho8"], scale=8.0)

        pc = [0]

        def wload(dst, src, ncol, scale_ap=None):
            s_ = pc[0] % 2; pc[0] += 1
            S.dma("sp", stg[s_][:, 0:ncol], src, writes=[f"stg{s_}"], group=f"stg{s_}")
            if scale_ap is None:
                ACT(dst, stg[s_][:, 0:ncol], AF.Copy, [f"stg{s_}"], ["wts"])
            else:
                ACT(dst, stg[s_][:, 0:ncol], AF.Copy, [f"stg{s_}", "prm"], ["wts"], scale=scale_ap)
        for k in range(8):
            for hh in range(2):
                wload(win[:, k, hh * SW:(hh + 1) * SW], w_in[k * 128:(k + 1) * 128, hh * SW:(hh + 1) * SW], SW, prm[:, k:k + 1])
        for k in range(8):
            wload(wkvb[:, k, :], w_kv[k * 128:(k + 1) * 128, :], 512, prm[:, 8 + k:9 + k])
        for k in range(8):
            wload(wout[:, k, :], w_out[k * 128:(k + 1) * 128, :], 1024)
        for k in range(3):
            wload(wglu[:, k, :], w_glu[k * 128:(k + 1) * 128, :], 384)

        if do_ssm:
            tt(9, 7, 7, ALU.mult); ts(8, 9, -2.0, 1.0, ALU.mult, ALU.add)
            for _ in range(3):
                tt(9, 8, 8, ALU.mult); tt(10, 6, 6, ALU.mult); tt(11, 8, 6, ALU.mult)
                tt(8, 9, 10, ALU.subtract); ts(6, 11, 2.0, None, ALU.mult)
            tt(13, 12, 8, ALU.mult); tt(14, 12, 6, ALU.mult)
            tt(16, 15, 8, ALU.mult); tt(25, 15, 6, ALU.mult); ts(17, 25, -1.0, None, ALU.mult)
            for tab, bre_, bim_, nd in ((PC, 13, 14, 3), (RR, 16, 17, 4)):
                tn = "PC" if tab is PC else "RR"
                op("dve", lambda e, tab=tab: e.memset(tab[:, 0, :, 0], 1.0), [], [tn])
                op("dve", lambda e, tab=tab: e.memset(tab[:, 1, :, 0], 0.0), [], [tn])
                CP("dve", tab[:, 0, :, 1], sm_(bre_), [sn(bre_)], [tn])
                CP("dve", tab[:, 1, :, 1], sm_(bim_), [sn(bim_)], [tn])
                for kk in range(nd):
                    n = 1 << kk
                    tv = lambda t_: t_[:, 0:G * n].rearrange("p (g m) -> p g m", m=n)
                    cmul("dve", tab[:, 0, :, 1 + n:1 + 2 * n], tab[:, 1, :, 1 + n:1 + 2 * n],
                         tab[:, 0, :, 1:1 + n], tab[:, 1, :, 1:1 + n],
                         tab[:, 0, :, n:n + 1].to_broadcast([128, G, n]), tab[:, 1, :, n:n + 1].to_broadcast([128, G, n]),
                         tv(tA), tv(tB), [tn], [tn], "tAB")
            tv15 = lambda t_: t_[:, 0:G * 15].rearrange("p (g m) -> p g m", m=15)
            cmul("dve", PB[:, 0], PB[:, 1], RR[:, 0, :, 0:15], RR[:, 1, :, 0:15],
                 PC[:, 0, :, 7:8].to_broadcast([128, G, 15]), PC[:, 1, :, 7:8].to_broadcast([128, G, 15]),
                 tv15(tA), tv15(tB), ["PC", "RR"], ["PB"], "tAB")
            ts(18, 13, -1.0, None, ALU.add)
            tt(9, 18, 0, ALU.mult); tt(10, 14, 1, ALU.mult); tt(20, 9, 10, ALU.add)
            tt(9, 14, 0, ALU.mult); tt(10, 18, 1, ALU.mult); tt(21, 9, 10, ALU.subtract)
            tt(9, 0, 0, ALU.mult); tt(10, 1, 1, ALU.mult); tt(19, 9, 10, ALU.add)
            op("dve", lambda e: e.reciprocal(sm_(26), sm_(19)), ["sm19"], ["sm26"])
            tt(22, 20, 26, ALU.mult); tt(23, 21, 26, ALU.mult)
            b3 = lambda ap: ap.unsqueeze(2).to_broadcast([128, G, 16])
            B3 = lambda i: Bt[:, i, :].rearrange("p (g h) -> p g h", h=16)
            C3 = lambda i: Ct[:, i, :].rearrange("p (g h) -> p g h", h=16)
            v3 = lambda t_: t_[:, 0:G * 16].rearrange("p (g h) -> p g h", h=16)
            cmul("dve", BB[:, 0], BB[:, 1], b3(sm_(22)), b3(sm_(23)), B3(0), B3(1), v3(tA), v3(tB),
                 ["sm22", "sm23", "Bt"], ["BB"], "tAB")
            mT, mB, mBn, mTn = sel[:, 0:1], sel[:, 1:2], sel[:, 2:3], sel[:, 4:5]

            def stack(dst, top, top_m, bot, bot_m, r):
                TS("dve", dst, top, top_m, None, ALU.mult, ALU.bypass, r + ["sel"], ["XY"])
                STT("dve", dst, bot, bot_m, dst, ALU.mult, ALU.add, r + ["sel", "XY"], ["XY"])
            stack(XY[:, 0], BB[:, 0], mT, BB[:, 1], mB, ["BB"])
            stack(XY[:, 1], BB[:, 1], mTn, BB[:, 0], mB, ["BB"])
            stack(XY[:, 2], BB[:, 1], mT, BB[:, 0], mB, ["BB"])
            stack(XY[:, 3], BB[:, 0], mT, BB[:, 1], mBn, ["BB"])
            stack(XY[:, 4], C3(0), mT, C3(1), mBn, ["Ct"])
            stack(XY[:, 5], C3(1), mTn, C3(0), mBn, ["Ct"])
            TT("dve", sm_(28), PC[:, 0, :, 8], sm_(24), ALU.mult, ["PC", "sm24"], ["sm28"])
            TT("dve", sm_(27), PC[:, 1, :, 8], sm_(24), ALU.mult, ["PC", "sm24"], ["sm27"])
            ts(29, 27, -1.0, None, ALU.mult)
            GH = G // 4
            for gh in range(3):
                gsl = slice(gh * GH, (gh + 1) * GH)
                en = "pool"
                ETre, ETim = (ETre0, ETim0)
                trn, tin = ("ETre0", "ETim0")
                tE_, tF_, tnm = (tE, tF, "tEF")
                CP(en, ETre[:, :, 0], sm[:, 28, gsl], ["sm28"], [trn])
                CP(en, ETim[:, :, 0], sm[:, 29, gsl], ["sm29"], [tin])
                for kk in range(7):
                    n = 1 << kk
                    tv = lambda t_: t_[:, 0:GH * n].rearrange("p (g m) -> p g m", m=n)
                    cmul(en, ETre[:, :, n:2 * n], ETim[:, :, n:2 * n], ETre[:, :, 0:n], ETim[:, :, 0:n],
                         ETre[:, :, n - 1:n].to_broadcast([128, GH, n]), ETim[:, :, n - 1:n].to_broadcast([128, GH, n]),
                         tv(tE_), tv(tF_), [trn, tin], [trn, tin], tnm)
                TS(en, ESp[:, gsl, :].rearrange("p g m -> p (g m)"), ETim[:].rearrange("p g m -> p (g m)"), sel[:, 3:4],
                   0.0, ALU.mult, ALU.add, [tin, "sel"], ["ESp"])
                CP(en, ECt[:, gsl, :].rearrange("p g m -> p (g m)"), ETre[:].rearrange("p g m -> p (g m)"), [trn], ["ECt"])
            for c in range(G // GC):
                gs = slice(GC * c, GC * c + GC)
                v4 = lambda t_, ni: t_[:, 0:GC * ni * 16].rearrange("p (g i h) -> p g i h", g=GC, i=ni)

                def prod(dst, tab, lo, ni, xa, xb_, res):
                    sh = [128, GC, ni, 16]
                    TT("dve", v4(tA, ni), tab[:, 0, gs, lo:lo + ni].unsqueeze(3).to_broadcast(sh),
                       XY[:, xa, gs, :].unsqueeze(2).to_broadcast(sh), ALU.mult, ["PB", "PC", "XY"], ["tAB1"])
                    TT("dve", v4(tB, ni), tab[:, 1, gs, lo:lo + ni].unsqueeze(3).to_broadcast(sh),
                       XY[:, xb_, gs, :].unsqueeze(2).to_broadcast(sh), ALU.mult, ["PB", "PC", "XY"], ["tAB2"])
                    TT("dve", dst, v4(tA, ni), v4(tB, ni), ALU.add, ["tAB1", "tAB2"], [res])
                prod(EBst[:], PB, 0, 15, 0, 1, "EBst")
                prod(ECst[:], PC, 0, 9, 4, 5, "ECst")
                CP("dve", Wc[:, gs, :].rearrange("p g (t h) -> p g t h", t=8), ECst[:, :, 1:9, :], ["ECst"], ["Wc"])
                W = GC * 128
                for gl in range(GC):
                    TR(pA[:, gl * 128:(gl + 1) * 128], EBst[:, gl, 0:8, :], identf[:], ["EBst", "identf"], ["pA0"])
                CP("dve", Wl[:, gs, :], pA[:, 0:W].rearrange("p (g n) -> p g n", g=GC), ["pA0"], ["Wl"])
                CP("dve", Wls[:, gs, 0:64], Wl[:, gs, 64:128], ["Wl"], ["Wls"])
                CP("dve", Wls[:, gs, 64:128], Wl[:, gs, 0:64], ["Wl"], ["Wls"])
                for gl in range(GC):
                    MM(pC[:, gl * 128:(gl + 1) * 128], EBst[:, gl, 7:15, :], ECst[:, gl, 0:8, :], True, True,
                       ["EBst", "ECst"], ["pC0"])
                m3 = lambda t_: t_[:, 0:W].rearrange("p (g n) -> p g n", g=GC)
                TT("dve", m3(tA), pC[:, 0:W].rearrange("p (g n) -> p g n", g=GC),
                   maskf[:].unsqueeze(1).to_broadcast([128, GC, 128]), ALU.mult, ["pC0", "maskf"], ["tAB1"])
                TT("dve", m3(tC), identf[:].unsqueeze(1).to_broadcast([128, GC, 128]),
                   dcs[:, gs].unsqueeze(2).to_broadcast([128, GC, 128]), ALU.mult, ["identf", "dcs"], ["tC"])
                TT("dve", MI[:, gs, :], m3(tA), m3(tC), ALU.add, ["tAB1", "tC"], ["MI"])
            for gh in range(3, 4):
                gsl = slice(gh * GH, (gh + 1) * GH)
                en = "dve"
                ETre, ETim = (ETre1, ETim1)
                trn, tin = ("EBst", "tAB1")
                tE_, tF_, tnm = (tB[:, 0:384], tB[:, 384:768], "tAB2_")
                CP(en, ETre[:, :, 0], sm[:, 28, gsl], ["sm28"], [trn])
                CP(en, ETim[:, :, 0], sm[:, 29, gsl], ["sm29"], [tin])
                for kk in range(7):
                    n = 1 << kk
                    tv = lambda t_: t_[:, 0:GH * n].rearrange("p (g m) -> p g m", m=n)
                    cmul(en, ETre[:, :, n:2 * n], ETim[:, :, n:2 * n], ETre[:, :, 0:n], ETim[:, :, 0:n],
                         ETre[:, :, n - 1:n].to_broadcast([128, GH, n]), ETim[:, :, n - 1:n].to_broadcast([128, GH, n]),
                         tv(tE_), tv(tF_), [trn, tin], [trn, tin], tnm)
                TS(en, ESp[:, gsl, :].rearrange("p g m -> p (g m)"), ETim[:].rearrange("p g m -> p (g m)"), sel[:, 3:4],
                   0.0, ALU.mult, ALU.add, [tin, "sel"], ["ESp"])
                CP(en, ECt[:, gsl, :].rearrange("p g m -> p (g m)"), ETre[:].rearrange("p g m -> p (g m)"), [trn], ["ECt"])
        act_only[0] = True
        pre_tiles([(mem[mc * 128:(mc + 1) * 128, :], memT[:, :, mc * 128:(mc + 1) * 128], "memT") for mc in range(2)])
        act_only[0] = False
        for fc in range(2):
            for k in range(8):
                MM(pA[:, 512:768], wkvb[:, k, fc * 128:(fc + 1) * 128], memT[:, k, :], k == 0, k == 7,
                   ["wts", "memT"], ["pA1"])
            CP("act", kT[:, fc, :], pA[:, 512:768], ["pA1"], ["kT"])
        for mc in range(2):
            for k in range(8):
                MM(pB[:, 512:768], memT[:, k, mc * 128:(mc + 1) * 128], wkvb[:, k, 256:512], k == 0, k == 7,
                   ["wts", "memT"], ["pB1"])
            CP("act", vv[:, mc, :], pB[:, 512:768], ["pB1"], ["vv"])
        if do_ssm and stage > 1:
            pre_early([(xp[t * 128:(t + 1) * 128, :],) for t in range(4)], "p0")
        S.barrier()

    if stage <= 1:
        S.dma("sp", out[0:128, 0:256], gfs[:, 0:256], reads=["gfs"], writes=[], group="st")
        S.final_wait("st"); S.finish(); return nc
    hnT = A("hnT", [128, 8, 2 + 1024], BF16)
    Asb = A("Asb", [128, 1024], BF16); U8 = A("U8", [128, G, 128], BF16)
    b1 = A("b1", [128, 1024], BF16); b2 = A("b2", [128, 1024], BF16)
    rbf = A("rbf", [128, 1024], BF16); Sbuf = A("Sbuf", [128, G, 129], BF16)
    gT = A("gT", [128, 3, 1024], BF16)
    rend = A("rend", [128, 8], BF16); rsw = A("rsw", [128, 8], BF16); t3e = A("t3e", [128, 8], F32); t4e = A("t4e", [128, 8], F32)
    sa = A("sa", [128, 3, 512], BF16); sc = A("sc", [128, 3, 512], BF16); cbs = A("cbs", [128, 3, 512], BF16)
    ci3 = A("ci", [128, 3, 512], BF16); ub = A("ub", [128, 3, 514], BF16)
    qT = A("qT", [128, 2, 512], BF16); sq = A("sq", [128, 2, 512], BF16); pp4 = A("pp", [128, 4, 512], BF16); pp = pp4
    yT = A("yT", [128, 8, 512], BF16); rden = A("rden", [128, 512], F32); tmpb = A("tmpb", [128, 512], BF16)
    op("dve", lambda e: e.memset(Sbuf[:, :, 128], 0.0), [], ["Sb00", "Sb01", "Sb10", "Sb11", "Sb20", "Sb21"])
    if not do_ssm:
        op("dve", lambda e: e.memset(gT[:].rearrange("p c n -> p (c n)"), 0.0), [], ["gT"])

    HN = [f"hn{t}" for t in range(8)]

    def blk_items(src, base):
        return [(src[base + t * 128: base + (t + 1) * 128, :], hnT[:, :, 2 + t * 128: 2 + (t + 1) * 128], HN[t])
                for t in range(8)]

    def pre_block(src, base, key=None):
        pre_tiles(blk_items(src, base), key)

    def ssm_block(with_out, fill=None, hook=None):
        def emit_fill(n):
            if fill is None:
                return
            for _ in range(n):
                u_ = next(fill, None)
                if u_ is None:
                    return
                u_()

        def F1(c):
            gs = slice(8 * c, 8 * c + 8)
            for j8 in range(8):
                for k in range(8):
                    MM(pA[:, j8 * 128:(j8 + 1) * 128], hnT[:, k, 2 + j8: 2 + 1024: 8], win[:, k, c * 128:(c + 1) * 128],
                       k == 0, k == 7, HN + ["wts"], ["pA0", "pA1"])
            CP("act", Asb[:].rearrange("p (g j h) -> p j g h", g=8, j=8),
               pA[:].rearrange("p (j g h) -> p j g h", j=8, g=8), ["pA0", "pA1"], ["Asb"])
            for gl in range(8):
                TR(pT0[:, gl * 128:(gl + 1) * 128], Asb[:, gl * 128:(gl + 1) * 128], identb[:], ["Asb", "identb"], ["pT0"])
            CP("act", U8[:, gs, :], pT0[:].rearrange("p (g n) -> p g n", g=8), ["pT0"], [f"U8_{c}"])

        def F2(c, h):
            for gl in range(4):
                g = 8 * c + 4 * h + gl
                MM(pB[:, gl * 128:(gl + 1) * 128], Wl[:, g, :], U8[:, g, :], True, True, ["Wl", f"U8_{c}"], ["pB0"])
            for gl in range(4):
                g = 8 * c + 4 * h + gl
                MM(pB[:, 512 + gl * 128:512 + (gl + 1) * 128], Wls[:, g, :], U8[:, g, :], True, True, ["Wls", f"U8_{c}"], ["pB1"])

        def BC(c, h):
            g0 = 8 * c + 4 * h
            gs = slice(g0, g0 + 4)
            fs = slice(h * 512, (h + 1) * 512)
            b1n, b2n, rn = f"b1{h}", f"b2{h}", f"rbf{h}"
            ec = ECt[:, gs, :].rearrange("p g m -> p (g m)"); esp = ESp[:, gs, :].rearrange("p g m -> p (g m)")
            TT("dve", b1[:, fs], pB[:, 0:512], ec, ALU.mult, ["pB0", "ECt"], [b1n])
            TT("dve", b2[:, fs], pB[:, 512:1024], esp, ALU.mult, ["pB1", "ESp"], [b2n])
            TT("dve", b1[:, fs], b1[:, fs], b2[:, fs], ALU.add, [b1n, b2n], [b1n])
            for gl in range(4):
                g = g0 + gl
                cs = slice(h * 512 + gl * 128, h * 512 + (gl + 1) * 128)
                op("dve", lambda e, cs=cs, g=g: e.tensor_tensor_scan(
                    rbf[:, cs], rho8[:, g:g + 1].to_broadcast([128, 128]), b1[:, cs], send[:, g:g + 1], ALU.mult, ALU.add),
                   [b1n, "rho8", f"send{c}{h}"], [rn])
            MM(pB[:, 0:512], permb[:], rbf[:, fs], True, True, ["permb", rn], ["pB0"])
            TT("dve", b1[:, fs], rbf[:, fs], ec, ALU.mult, [rn, "ECt"], [b1n])
            TT("dve", b2[:, fs], pB[:, 0:512], esp, ALU.mult, ["pB0", "ESp"], [b2n])
            CP("pool", Sbuf[:, gs, 0:1], Sbuf[:, gs, 128:129], [f"Sb{c}{h}"], [f"Sb{c}{h}"])
            TT("dve", Sbuf[:, gs, 1:129], b1[:, fs].rearrange("p (g m) -> p g m", g=4),
               b2[:, fs].rearrange("p (g m) -> p g m", g=4), ALU.subtract, [b1n, b2n, f"Sb{c}{h}"], [f"Sb{c}{h}"])
            TT("pool", send[:, gs], b1[:, h * 512 + 127:(h + 1) * 512:128], b2[:, h * 512 + 127:(h + 1) * 512:128],
               ALU.subtract, [b1n, b2n], [f"send{c}{h}"])

        def F2full(c):
            for gl in range(8):
                g = 8 * c + gl
                MM(pB[:, gl * 128:(gl + 1) * 128], Wl[:, g, :], U8[:, g, :], True, True, ["Wl", f"U8_{c}"], ["pB0", "pB1"])
            for gl in range(8):
                g = 8 * c + gl
                MM(pC[:, gl * 128:(gl + 1) * 128], Wls[:, g, :], U8[:, g, :], True, True, ["Wls", f"U8_{c}"], ["pC0", "pC1"])

        def BCfull(c):
            gs = slice(8 * c, 8 * c + 8)
            sbn = [f"Sb{c}0", f"Sb{c}1"]; sdn = [f"send{c}0", f"send{c}1"]
            ec = ECt[:, gs, :].rearrange("p g m -> p (g m)"); esp = ESp[:, gs, :].rearrange("p g m -> p (g m)")
            TT("dve", b1[:], pB[:], ec, ALU.mult, ["pB0", "pB1", "ECt"], ["b10", "b11"])
            TT("dve", b2[:], pC[:], esp, ALU.mult, ["pC0", "pC1", "ESp"], ["b20", "b21"])
            TT("dve", b1[:], b1[:], b2[:], ALU.add, ["b10", "b11", "b20", "b21"], ["b10", "b11"])
            for gl in range(8):
                g = 8 * c + gl
                op("dve", lambda e, gl=gl, g=g: e.tensor_tensor_scan(
                    rbf[:, gl * 128:(gl + 1) * 128], rho8[:, g:g + 1].to_broadcast([128, 128]),
                    b1[:, gl * 128:(gl + 1) * 128], send[:, g:g + 1], ALU.mult, ALU.add),
                   ["b10", "b11", "rho8"] + sdn, ["rbf0", "rbf1"])
            CP("pool", rend[:], rbf[:, 127:1024:128], ["rbf0", "rbf1"], ["rend"])
            S.dma("sp", rsw[64:128, :], rend[0:64, :], reads=["rend"], writes=["rswA"], group="rswA")
            S.dma("sp", rsw[0:64, :], rend[64:128, :], reads=["rend"], writes=["rswB"], group="rswB")
            TT("pool", t3e[:], rend[:], ECt[:, gs, 127], ALU.mult, ["rend", "ECt"], ["t3e"])
            TT("pool", t4e[:], rsw[:], ESp[:, gs, 127], ALU.mult, ["rswA", "rswB", "ESp"], ["t4e"])
            TT("pool", send[:, gs], t3e[:], t4e[:], ALU.subtract, ["t3e", "t4e"], sdn)
            CP("pool", Sbuf[:, gs, 128], send[:, gs], sdn + sbn, sbn)

        def Dst(c):
            for gl in range(8):
                g = 8 * c + gl
                MM(pA[:, gl * 128:(gl + 1) * 128], U8[:, g, :], MI[:, g, :], True, False, [f"U8_{c}", "MI"], ["pA0", "pA1"])
                MM(pA[:, gl * 128:(gl + 1) * 128], Sbuf[:, g, 0:128], Wc[:, g, :], False, True,
                   [f"Sb{c}0", f"Sb{c}1", "Wc"], ["pA0", "pA1"])
            ACT(Asb[:].rearrange("p (t g h) -> p g t h", t=8, g=8),
                pA[:].rearrange("p (g t h) -> p g t h", g=8, t=8), AF.Gelu_apprx_tanh, ["pA0", "pA1"], ["Asb"])
            for t8 in range(8):
                TR(pT1[:, t8 * 128:(t8 + 1) * 128], Asb[:, t8 * 128:(t8 + 1) * 128], identb[:], ["Asb", "identb"], ["pT1"])
            CP("act", gT[:, c, :], pT1[:], ["pT1"], ["gT"])

        if fill is None and not with_out:
            F1(0); F2full(0); F1(1); BCfull(0); F2full(1); F1(2)
            if hook is not None:
                hook()
            BCfull(1); F2full(2); BCfull(2)
            return
        F1(0); F2(0, 0); emit_fill(3 if with_out else 0)
        for c in range(3):
            BC(c, 0); F2(c, 1); emit_fill(3 if c == 0 else 2)
            if c + 1 < 3:
                F1(c + 1); emit_fill(2 if c == 0 else 1)
            BC(c, 1)
            if c + 1 < 3:
                F2(c + 1, 0); emit_fill(2)
            if with_out:
                Dst(c); emit_fill(1)
        emit_fill(1000)

    bank_sets = {"all": [(pA, 0, "pA0"), (pA, 1, "pA1"), (pB, 0, "pB0"), (pB, 1, "pB1")],
                 "pc": [(pC, 0, "pC0"), (pC, 1, "pC1")]}
    pwc = [0]

    def mixer_units(blk, first, bs):
        banks = bank_sets[bs]
        c0 = 2 + blk * 512
        cols = slice(c0, c0 + 512)
        hres = HN[4 * blk:4 * blk + 4]

        def nextbank():
            t_, h_, nm_ = banks[pwc[0] % len(banks)]; pwc[0] += 1
            return t_[:, h_ * 512:(h_ + 1) * 512], nm_

        def projb(fi):
            bank, nm_ = nextbank()
            for k in range(8):
                MM(bank, win[:, k, fi * 128:(fi + 1) * 128], hnT[:, k, cols], k == 0, k == 7, hres + ["wts"], [nm_])
            return bank, nm_
        units = []
        if first:
            units.append(lambda: pre_tiles([(xh, tmp_halo, "pp0")]))

        def silu_unit(fi, dst, res):
            def u_():
                b_, n_ = projb(fi)
                ACT(dst, b_, AF.Silu, [n_], [res])
            return u_
        for i in range(3):
            units.append(silu_unit(3 + i, sa[:, i, :], "sa"))
        for i in range(3):
            units.append(silu_unit(15 + i, sc[:, i, :], "sc"))
        for i in range(2):
            units.append(silu_unit(20 + i, sq[:, i, :], "sq"))

        def conv_in(i):
            ci = ci3[:, i, :]
            cin = f"ci{i}"

            def u1():
                if first:
                    bh, nh = nextbank()
                    for k in range(8):
                        MM(bh[:, 0:2], win[:, k, (6 + i) * 128:(7 + i) * 128], tmp_halo[:, k, 0:2], k == 0, k == 7,
                           ["pp0", "wts"], [nh])
                    CP("act", ci[:, 0:2], bh[:, 0:2], [nh], [cin])
                    bh2, nh2 = nextbank()
                    for k in range(8):
                        MM(bh2[:, 0:2], win[:, k, (12 + i) * 128:(13 + i) * 128], tmp_halo[:, k, 0:2], k == 0, k == 7,
                           ["pp0", "wts"], [nh2])
                    TT("dve", ub[:, i, 0:2], bh2[:, 0:2], ci[:, 0:2], ALU.mult, [nh2, cin], [f"ub{i}"])
                else:
                    CP("pool", ub[:, i, 0:2], ub[:, i, 512:514], [f"ub{i}"], [f"ub{i}"])
                b_, n_ = projb(6 + i)
                CP("act", ci, b_, [n_], [cin])

            def u2():
                b_, n_ = projb(12 + i)
                CP("act", ub[:, i, 2:514], b_, [n_], [f"ub{i}"])

            def u2b():
                TT("dve", ub[:, i, 2:514], ub[:, i, 2:514], ci, ALU.mult, [f"ub{i}", cin], [f"ub{i}"])

            def u3():
                b_, n_ = projb(9 + i)
                CP("act", cbs[:, i, :], b_, [n_], ["cbs"])
            return [u1, u2, u3, u2b]
        late = []
        for i in range(3):
            u1_, u2_, u3_, u2b_ = conv_in(i)
            units += [u1_, u2_, u3_]
            late.append(u2b_)

        def q_unit(i):
            def u_():
                b_, n_ = projb(18 + i)
                CP("act", qT[:, i, :], b_, [n_], ["qT"])
            return u_
        for i in range(2):
            units.append(q_unit(i))

        def conv_unit(i):
            def u_():
                bank, nm_ = nextbank()
                for tap in range(3):
                    MM(bank, cdiag[:, i * 3 + tap, :], ub[:, i, tap:tap + 512], tap == 0, tap == 2, ["cdiag", f"ub{i}"], [nm_])
                STT("dve", tmpb[:], bank, prm[:, 22 + i:23 + i], cbs[:, i, :], ALU.add, ALU.mult,
                    [nm_, "prm", "cbs"], ["tmpb"])
                TT("dve", yT[:, 3 + i, :], tmpb[:], sc[:, i, :], ALU.mult, ["tmpb", "sc"], [f"yT{3 + i}"])
            return u_
        units += late
        for i in range(3):
            units.append(conv_unit(i))
        return units

    def mixer_rest(hf, blk):
        prefetch_x(hf, blk)
        gview = lambda c: gT[:, c, :].rearrange("p (j n) -> p j n", j=8)[:, :, blk * 64:(blk + 1) * 64]
        tmp2 = rden[:].bitcast(BF16)[:, 0:512]
        for oc in range(3):
            tb, tbn = (tmpb[:], "tmpb") if oc % 2 == 0 else (tmp2, "rden")
            t3v = tb.rearrange("p (j n) -> p j n", j=8)
            for c in range(3):
                MM(pC[:, (oc % 2) * 512:(oc % 2 + 1) * 512], wglu[:, c, oc * 128:(oc + 1) * 128], gview(c), c == 0, c == 2,
                   ["wts", "gT"], [f"pC{oc % 2}"])
            ACT(tb, pC[:, (oc % 2) * 512:(oc % 2 + 1) * 512], AF.Tanh, [f"pC{oc % 2}", "prm"], [tbn], scale=0.5,
                bias=prm[:, 19 + oc:20 + oc])
            STT("dve", t3v, t3v, 1.0, gview(oc), ALU.add, ALU.mult, [tbn, "gT"], [tbn])
            STT("dve", yT[:, oc, :].rearrange("p (n j) -> p n j", j=8), tb.rearrange("p (j n) -> p n j", j=8), 0.5,
                sa[:, oc, :].rearrange("p (n j) -> p n j", j=8), ALU.mult, ALU.mult, [tbn, "sa"], [f"yT{oc}"])
        sbank = {0: (pA, "pA"), 1: (pC, "pC")}

        def scores(h):
            pr, hh = divmod(h, 2)
            rows = slice(hh * 64, hh * 64 + 64)
            bt, bn = sbank[hh]
            for mc in range(2):
                MM(bt[:, mc * 512:(mc + 1) * 512], kT[rows, pr, mc * 128:(mc + 1) * 128], qT[rows, pr, :], True, True,
                   ["kT", "qT"], [f"{bn}{mc}"])

        def expo(h):
            hh = h % 2
            bt, bn = sbank[hh]
            for mc in range(2):
                ACT(pp4[:, 2 * hh + mc, :], bt[:, mc * 512:(mc + 1) * 512], AF.Exp, [f"{bn}{mc}"], [f"pp{hh}"], scale=0.125)

        def pv(h):
            pr, hh = divmod(h, 2)
            rows = slice(hh * 64, hh * 64 + 64)
            for mc in range(2):
                MM(pB[rows, 0:512], vv[:, mc, h * 64:(h + 1) * 64], pp4[:, 2 * hh + mc, :], mc == 0, mc == 1,
                   ["vv", f"pp{hh}"], ["pB0"])
            for mc in range(2):
                MM(pB[rows, 512:1024], onesb[:], pp4[:, 2 * hh + mc, :], mc == 0, mc == 1, ["onesb", f"pp{hh}"], ["pB1"])

        def fin(pr):
            ACT(rden[:], pB[:, 512:1024], AF.Ln, ["pB1"], ["rden"])
            ACT(rden[:], rden[:], AF.Exp, ["rden"], ["rden"], scale=-1.0)
            TT("dve", tmpb[:], pB[:, 0:512], rden[:], ALU.mult, ["pB0", "rden"], ["tmpb"])
            TT("dve", yT[:, 6 + pr, :], tmpb[:], sq[:, pr, :], ALU.mult, ["tmpb", "sq"], [f"yT{6 + pr}"])
        scores(0); expo(0); scores(1); expo(1); pv(0); scores(2); pv(1); fin(0)
        expo(2); scores(3); expo(3); pv(2); pv(3); fin(1)
        tiles = []
        for j in range(4):
            i = tilectr[0]; tilectr[0] += 1
            tiles.append((i % NXB, hf * 1024 + blk * 512 + j * 128))
        for p_ in range(2):
            pair = [(j, tiles[j]) for j in (2 * p_, 2 * p_ + 1)]
            for j, (sl, tok) in pair:
                xb = xbuf[sl]; xn = f"xb{sl}"
                pO, pOn = (pC, "pC") if j % 2 == 0 else (pB, "pB")
                for nh in range(2):
                    for kc in range(8):
                        MM(pO[:, nh * 512:(nh + 1) * 512], yT[:, kc, j * 128:(j + 1) * 128], wout[:, kc, nh * 512:(nh + 1) * 512],
                           kc == 0, kc == 7, [f"yT{kc}", "wts"], [f"{pOn}{nh}"])
                TT("dve", xb[:], pO[:], xb[:], ALU.add, [f"{pOn}0", f"{pOn}1", xn], [xn])
                ACT(junkD, xb[:], AF.Square, [xn], ["junk1", f"ssqB{p_}"], accum_out=stat2[:, j:j + 1])
            c0_ = 2 * p_
            ACT(stat2[:, 4 + c0_:6 + c0_], stat2[:, c0_:c0_ + 2], AF.Ln, [f"ssqB{p_}", "epsc"], [f"vB{p_}"], scale=1.0 / D,
                bias=epsc[:, 0:1])
            ACT(stat2[:, 8 + c0_:10 + c0_], stat2[:, 4 + c0_:6 + c0_], AF.Exp, [f"vB{p_}"], [f"rsB{p_}"], scale=-0.5)
            for j, (sl, tok) in pair:
                xb = xbuf[sl]; xn = f"xb{sl}"
                STT("dve", xb[:], xb[:], stat2[:, 8 + j:9 + j], gfs[:], ALU.mult, ALU.mult, [xn, f"rsB{p_}", "gfs"], [xn])
                S.dma("sp", out[tok:tok + 128, :], xb[:], reads=[xn], writes=[], group=f"st{sl}")

    def prefetch_x(hf, blk):
        for j in range(4):
            sl = (tilectr[0] + j) % NXB
            tok = hf * 1024 + blk * 512 + j * 128
            S.dma("sp", xbuf[sl][:], xs[tok:tok + 128, :], writes=[f"xb{sl}"], group=f"xb{sl}")

    tmp_halo = pp4[:, 0:2, :].rearrange("p a (b n) -> p (a b) n", b=4)

    if do_ssm:
        pre_block(xp, 0, "p0")
        ssm_block(False, hook=lambda: pre_early(blk_items(xp, 1024), "p1"))
        pre_block(xp, 1024, "p1")
        ssm_block(False, hook=lambda: pre_early(blk_items(xs, 0), "o0"))
    if stage == 2:
        S.dma("sp", out[0:128, 0:256], gfs[:, 0:256], reads=["Sbuf", "send", "gfs"], writes=[], group="st")
        S.final_wait("st"); S.finish(); return nc
    for hf in range(2):
        pre_block(xs, hf * 1024, "o0" if (hf == 0 and do_ssm) else None)
        u0 = mixer_units(0, hf == 0, "pc" if do_ssm else "all")
        if do_ssm:
            ssm_block(True, iter(u0))
        else:
            for u_ in u0:
                u_()
        mixer_rest(hf, 0)
        for u_ in mixer_units(1, False, "all"):
            u_()
        mixer_rest(hf, 1)
    for i_ in range(NXB):
        S.final_wait(f"st{i_}")
    S.finish()
    return nc


def kernel(**inputs):
    f = lambda a: np.ascontiguousarray(np.asarray(a, dtype=np.float32))
    x = f(inputs["x"]); mem = f(inputs["mem"])
    t128 = lambda a: np.ascontiguousarray(np.tile(a, (2,) + (1,) * (a.ndim - 1)))
    lam_re = f(inputs["ssm_lam_re"])[0]; lam_im = f(inputs["ssm_lam_im"])[0]
    shared = {
        "w_in": f(inputs["w_in"])[0], "w_out": f(inputs["w_out"])[0],
        "w_glu": f(inputs["w_glu"])[0], "w_kv": f(inputs["w_mem_kv"])[0],
        "gin": f(f(inputs["norm_g"])[0].reshape(8, 128).T),
        "gmem": f(f(inputs["mem_norm_g"])[0].reshape(8, 128).T),
        "gfin": f(np.tile(f(inputs["final_norm_g"]).reshape(1, D), (128, 1))),
        "bglu": f(f(inputs["b_glu"])[0].reshape(3, 128).T),
        "cw": f(f(inputs["conv_w"])[0].reshape(3, 3, 128).transpose(2, 1, 0).reshape(128, 9)),
        "cbias": f(f(inputs["conv_b"])[0].reshape(3, 128).T),
        "lre": t128(f(lam_re.T)), "lim": t128(f(lam_im.T)),
        "ldt": f(np.tile(f(inputs["ssm_log_dt"])[0].reshape(1, G), (128, 1))),
        "bre": t128(f(f(inputs["ssm_b_re"])[0].transpose(1, 0, 2).reshape(64, G * 16))),
        "bim": t128(f(f(inputs["ssm_b_im"])[0].transpose(1, 0, 2).reshape(64, G * 16))),
        "cre": t128(f(f(inputs["ssm_c_re"])[0].transpose(2, 0, 1).reshape(64, G * 16))),
        "cim": t128(f(f(inputs["ssm_c_im"])[0].transpose(2, 0, 1).reshape(64, G * 16))),
        "dcol": f(np.tile(f(inputs["ssm_d"])[0].reshape(G, 16).T, (8, 1))),
        "ident": np.eye(128, dtype=np.float32),
        "mask": f(np.kron(np.triu(np.ones((8, 8), np.float32)), np.ones((16, 16), np.float32))),
        "perm": f(np.roll(np.eye(128, dtype=np.float32), 64, axis=1)),
        "sel": f(np.stack([np.r_[np.ones(64), np.zeros(64)], np.r_[np.zeros(64), np.ones(64)],
                           np.r_[np.zeros(64), -np.ones(64)], np.r_[-np.ones(64), np.ones(64)],
                           np.r_[-np.ones(64), np.zeros(64)], np.r_[np.ones(64), np.ones(64)]], axis=1)),
    }
    in_maps = []
    for b in range(4):
        for s in range(2):
            m = dict(shared)
            m["xs"] = f(x[b, s * NT:(s + 1) * NT])
            m["xp"] = f(x[b, 0:NT]) if s == 1 else np.zeros((NT, D), np.float32)
            xh = np.zeros((128, D), np.float32)
            if s == 1:
                xh[0:2] = x[b, NT - 2:NT]
            m["xh"] = xh
            m["mem"] = f(mem[b])
            in_maps.append(m)
    nc = build_nc(stage=STAGE)
    res = run_bass_kernel_spmd(nc, in_maps, core_ids=list(range(8)))
    outp = np.empty((4, 2 * NT, D), np.float32)
    for b in range(4):
        for s in range(2):
            outp[b, s * NT:(s + 1) * NT] = res.results[2 * b + s]["out"]
    return outp
```

```python
from contextlib import ExitStack

import numpy as np
import concourse.bass as bass
import concourse.mybir as mybir
from concourse.bass_utils import run_bass_kernel_spmd

F32 = mybir.dt.float32
BF16 = mybir.dt.bfloat16
AF = mybir.ActivationFunctionType
ALU = mybir.AluOpType


class _Op:
    __slots__ = ("eng", "fn", "deps", "signaled", "seq", "dma_group", "dma_val", "idx")

    def __init__(self, eng, fn):
        self.eng = eng
        self.fn = fn
        self.deps = []
        self.signaled = False
        self.seq = 0
        self.dma_group = None
        self.dma_val = 0
        self.idx = 0


class Sched:
    ENG = ("pe", "act", "dve", "pool", "sp")

    def __init__(self, nc):
        self.nc = nc
        self.ops = []
        self.byeng = {e: [] for e in self.ENG}
        self.res = {}
        self.groups = {}
        self.final = []
        self.pending = {}

    def _track(self, op, reads, writes):
        deps = {}

        def add(o, kind):
            if o is None or o is op:
                return
            if o.dma_group is not None and o.dma_group == op.dma_group and self.groups[o.dma_group]["bulk"]:
                return
            if o.eng == op.eng and o.dma_group is None:
                if op.eng == "pe":
                    return
            deps[o.idx] = o

        for r in reads:
            st = self.res.get(r)
            if st is not None:
                add(st[0], "raw")
        for w in writes:
            st = self.res.get(w)
            if st is not None:
                add(st[0], "waw")
                for rd in st[1]:
                    add(rd, "war")
        for r in reads:
            st = self.res.setdefault(r, [None, []])
            st[1].append(op)
        for w in writes:
            self.res[w] = [op, []]
        op.deps = list(deps.values())
        for o in op.deps:
            o.signaled = True

    def op(self, eng, fn, reads=(), writes=(), _grp=None):
        o = _Op(eng, fn)
        o.dma_group = _grp
        o.idx = len(self.ops)
        self.ops.append(o)
        self.byeng[eng].append(o)
        self._track(o, reads, writes)
        pend = self.pending.pop(eng, None)
        if pend:
            have = {d.idx for d in o.deps}
            for d in pend:
                if d.idx not in have and not (d.eng == eng and d.dma_group is None):
                    o.deps.append(d)
                    d.signaled = True
        return o

    def barrier(self):
        last = []
        for e in ("pe", "act", "dve", "pool"):
            ops = [o for o in self.byeng[e] if o.dma_group is None]
            if ops:
                last.append(ops[-1])
        seen = {}
        for o in self.ops:
            if o.dma_group is not None:
                seen[o.dma_group] = o
        last += list(seen.values())
        self.pending = {e: list(last) for e in self.ENG}

    def dma(self, eng, out, in_, reads=(), writes=(), group="misc", bulk=False, **kw):
        g = self.groups.setdefault(group, {"n": 0, "bulk": bulk})
        g["n"] += 1
        o = self.op(eng, lambda e: e.dma_start(out=out, in_=in_, **kw), reads, writes, _grp=group)
        o.dma_val = 16 * g["n"]
        return o

    def final_wait(self, group):
        self.final.append(group)

    def finish(self):
        nc = self.nc
        with ExitStack() as st:
            sems = {}
            for e in ("pe", "act", "dve", "pool"):
                sems[e] = st.enter_context(nc.semaphore("s_" + e))
            for gname in self.groups:
                sems["g_" + gname] = st.enter_context(nc.semaphore("g_" + gname))
            for e in ("pe", "act", "dve", "pool"):
                c = 0
                for o in self.byeng[e]:
                    if o.dma_group is None and o.signaled:
                        c += 1
                        o.seq = c

            def token(o):
                if o.dma_group is not None:
                    g = self.groups[o.dma_group]
                    v = 16 * g["n"] if g["bulk"] else o.dma_val
                    return "g_" + o.dma_group, v
                return o.eng, o.seq

            block = st.enter_context(nc.Block())

            def run(eng_name):
                def body(e):
                    waited = {}
                    for o in self.byeng[eng_name]:
                        for d in o.deps:
                            k, v = token(d)
                            if waited.get(k, 0) < v:
                                e.wait_ge(sems[k], v)
                                waited[k] = v
                        ins = o.fn(e)
                        if o.dma_group is not None:
                            ins.then_inc(sems["g_" + o.dma_group], 16)
                        elif o.signaled:
                            ins.then_inc(sems[eng_name], 1)
                    if eng_name == "sp":
                        for gname in self.final:
                            e.wait_ge(sems["g_" + gname], 16 * self.groups[gname]["n"])
                return body

            block.tensor(run("pe"))
            block.scalar(run("act"))
            block.vector(run("dve"))
            block.gpsimd(run("pool"))
            block.sync(run("sp"))


G, P = 24, 64
STAGE = 9
D, DIN, NT = 1024, 2816, 2048
EPS = 1e-6


def build_nc(do_ssm=True, stage=9):
    nc = bass.Bass("TRN2", target_bir_lowering=False)
    S = Sched(nc)

    def dram(name, shape, kind="ExternalInput"):
        return nc.dram_tensor(name, list(shape), F32, kind=kind).ap()

    xs = dram("xs", [NT, D]); xp = dram("xp", [NT, D]); xh = dram("xh", [128, D])
    mem = dram("mem", [256, D])
    w_in = dram("w_in", [D, DIN]); w_out = dram("w_out", [D, D])
    w_glu = dram("w_glu", [384, 384]); w_kv = dram("w_kv", [D, 512])
    gin = dram("gin", [128, 8]); gmem = dram("gmem", [128, 8]); gfin = dram("gfin", [128, D])
    bglu = dram("bglu", [128, 3]); cw = dram("cw", [128, 9]); cbias = dram("cbias", [128, 3])
    lre = dram("lre", [128, G]); lim = dram("lim", [128, G]); ldt = dram("ldt", [128, G])
    bre = dram("bre", [128, G * 16]); bim = dram("bim", [128, G * 16])
    cre = dram("cre", [128, G * 16]); cim = dram("cim", [128, G * 16])
    dcol = dram("dcol", [128, G])
    cident = dram("ident", [128, 128]); cmask = dram("mask", [128, 128]); cperm = dram("perm", [128, 128])
    csel = dram("sel", [128, 6])
    out = dram("out", [NT, D], kind="ExternalOutput")

    A = nc.alloc_sbuf_tensor
    win = A("win", [128, 8, DIN], BF16); wout = A("wout", [128, 8, D], BF16)
    wglu = A("wglu", [128, 3, 384], BF16)
    kT = A("kT", [128, 2, 256], BF16); vv = A("vv", [128, 2, 256], BF16)
    identf = A("identf", [128, 128], F32); identb = A("identb", [128, 128], BF16)
    maskf = A("maskf", [128, 128], F32); permf = A("permf", [128, 128], F32); permb = A("permb", [128, 128], BF16)
    onesb = A("onesb", [128, 64], BF16)
    sel = A("sel_sb", [128, 6], F32)
    gfs = A("gfs", [128, D], F32)
    prm = A("prm", [128, 32], F32)
    cws = A("cws", [128, 9], F32); cdiag = A("cdiag", [128, 9, 128], BF16)
    MI = A("MI", [128, G, 128], BF16); Wl = A("Wl", [128, G, 128], BF16)
    Wls = A("Wls", [128, G, 128], BF16); Wc = A("Wc", [128, G, 128], BF16)
    ECt = A("ECt", [128, G, 128], BF16); ESp = A("ESp", [128, G, 128], BF16)
    rho8 = A("rho8", [128, G], F32); send = A("send", [128, G], F32)
    stat = A("stat", [128, 16], F32); stat2 = A("stat2", [128, 12], F32); mhalf = A("mhalf", [128, 1], F32); epsc = A("epsc", [128, 1], F32)
    pA = nc.alloc_psum_tensor("pA", [128, 1024], F32); pB = nc.alloc_psum_tensor("pB", [128, 1024], F32)
    pC = nc.alloc_psum_tensor("pC", [128, 1024], F32)
    pT0 = nc.alloc_psum_tensor("pT0", [128, 1024], BF16); pT1 = nc.alloc_psum_tensor("pT1", [128, 1024], BF16)

    op = S.op
    TT = lambda eng, o, a, b, alu, r, w: op(eng, lambda e: e.tensor_tensor(o, a, b, alu), r, w)
    TS = lambda eng, o, a, s1, s2, o0, o1, r, w: op(eng, lambda e: e.tensor_scalar(o, a, s1, s2, o0, o1), r, w)
    STT = lambda eng, o, a, s, b, o0, o1, r, w: op(eng, lambda e: e.scalar_tensor_tensor(o, a, s, b, o0, o1), r, w)
    ACT = lambda o, a, f, r, w, **kw: op("act", lambda e: e.activation(o, a, f, **kw), r, w)
    CP = lambda eng, o, a, r, w: (op(eng, lambda e: e.tensor_copy(o, a), r, w) if eng != "act"
                                  else op(eng, lambda e: e.activation(o, a, AF.Copy), r, w))
    MM = lambda o, l, rh, st, sp, r, w: op("pe", lambda e: e.matmul(o, l, rh, start=st, stop=sp), r, w)
    TR = lambda o, i, idn, r, w: op("pe", lambda e: e.transpose(o, i, idn), r, w)

    ldgrp = ["c"]

    def ld(dst, src, name):
        S.dma("sp", dst, src, writes=[name], group=ldgrp[0], bulk=True)
    ld(identf[:], cident, "identf"); ld(maskf[:], cmask, "maskf"); ld(permf[:], cperm, "permf")
    ld(sel[:], csel, "sel"); ld(gfs[:], gfin, "gfs")
    ld(prm[:, 0:8], gin, "prm"); ld(prm[:, 8:16], gmem, "prm"); ld(prm[:, 16:19], bglu, "prm")
    ld(prm[:, 22:25], cbias, "prm"); ld(cws[:], cw, "cws")
    CP("dve", identb[:], identf[:], ["identf"], ["identb"])
    CP("dve", permb[:], permf[:], ["permf"], ["permb"])
    op("dve", lambda e: e.memset(onesb[:], 1.0), [], ["onesb"])
    op("dve", lambda e: e.memset(mhalf[:], -0.5), [], ["mhalf"])
    op("dve", lambda e: e.memset(epsc[:], EPS), [], ["epsc"])
    op("dve", lambda e: e.memset(send[:], 0.0), [], ["send00", "send01", "send10", "send11", "send20", "send21"])
    TS("dve", prm[:, 19:22], prm[:, 16:19], 0.5, None, ALU.mult, ALU.bypass, ["prm"], ["prm"])
    for i in range(9):
        TS("dve", cdiag[:, i, :], identf[:], cws[:, i:i + 1], None, ALU.mult, ALU.bypass,
           ["identf", "cws"], ["cdiag"])

    NXB = 4
    xbuf = [A(f"xb{i}", [128, D], F32) for i in range(NXB)]
    xsbs = [A(f"xsb{i}", [128, D], BF16) for i in range(2)]
    junk1 = A("junk1", [128, 2], BF16)
    junkD = junk1[:, 0:1].to_broadcast([128, D])
    tilectr = [0]

    def rstd_from(ssq_ap, v_ap, r_ap, names):
        TS("dve", v_ap, ssq_ap, 1.0 / D, EPS, ALU.mult, ALU.add, [names[0]], [names[1]])
        ACT(v_ap, v_ap, AF.Sqrt, [names[1]], [names[1]])
        op("dve", lambda e: e.reciprocal(r_ap, v_ap), [names[1]], [names[2]])

    def pre_a(src):
        i = tilectr[0]; tilectr[0] += 1
        sl = i % NXB; st_ = i % 4
        S.dma("sp", xbuf[sl][:], src, writes=[f"xb{sl}"], group=f"xb{sl}")
        ACT(junkD, xbuf[sl][:], AF.Square, [f"xb{sl}"], ["junk1", f"ssq{st_}"], accum_out=stat[:, st_:st_ + 1])
        return i

    act_only = [False]

    def pre_b(i):
        sl = i % NXB; st_ = i % 4; xn = f"xb{sl}"
        nm = [f"ssq{st_}", f"v{st_}", f"rs{st_}"]
        xs_ = xsbs[i % 2]; xsn = f"xsb{i % 2}"
        if act_only[0]:
            ACT(stat[:, 4 + st_:5 + st_], stat[:, st_:st_ + 1], AF.Ln, [nm[0], "epsc"], [nm[1]], scale=1.0 / D, bias=epsc[:, 0:1])
            ACT(stat[:, 8 + st_:9 + st_], stat[:, 4 + st_:5 + st_], AF.Exp, [nm[1]], [nm[2]], scale=-0.5)
            ACT(xs_[:], xbuf[sl][:], AF.Copy, [xn, nm[2]], [xsn], scale=stat[:, 8 + st_:9 + st_])
        else:
            ACT(stat[:, 4 + st_:5 + st_], stat[:, st_:st_ + 1], AF.Ln, [nm[0], "epsc"], [nm[1]], scale=1.0 / D, bias=epsc[:, 0:1])
            ACT(stat[:, 8 + st_:9 + st_], stat[:, 4 + st_:5 + st_], AF.Exp, [nm[1]], [nm[2]], scale=-0.5)
            TS("dve", xs_[:], xbuf[sl][:], stat[:, 8 + st_:9 + st_], None, ALU.mult, ALU.bypass, [xn, nm[2]], [xsn])
        pt = pT0 if i % 2 == 0 else pT1; pn = "pT0" if i % 2 == 0 else "pT1"
        for k in range(8):
            TR(pt[:, k * 128:(k + 1) * 128], xs_[:, k * 128:(k + 1) * 128], identb[:], [xsn, "identb"], [pn])

    def pre_c(i, dst3, dst_res):
        pt = pT0 if i % 2 == 0 else pT1; pn = "pT0" if i % 2 == 0 else "pT1"
        CP("act" if act_only[0] else "dve", dst3, pt[:].rearrange("p (k n) -> p k n", k=8), [pn], [dst_res])

    early = {"ids": [], "key": None}

    def pre_early(items, key, n_early=4):
        early["ids"] = [pre_a(items[t][0]) for t in range(n_early)]
        early["key"] = key

    def pre_tiles(items, key=None):
        n = len(items); ids = []
        if key is not None and early["key"] == key:
            ids = list(early["ids"]); early["ids"] = []; early["key"] = None
        n0 = len(ids)
        for t in range(n + 2):
            if n0 <= t < n:
                ids.append(pre_a(items[t][0]))
            if 1 <= t <= n:
                pre_b(ids[t - 1])
            if t >= 2:
                pre_c(ids[t - 2], items[t - 2][1], items[t - 2][2])

    def cmul(eng, o_re, o_im, a_re, a_im, b_re, b_im, t1, t2, r, w, tn):
        TT(eng, t1, a_re, b_re, ALU.mult, r, [tn + "1"])
        TT(eng, t2, a_im, b_im, ALU.mult, r, [tn + "2"])
        TT(eng, o_re, t1, t2, ALU.subtract, [tn + "1", tn + "2"], w)
        TT(eng, t1, a_re, b_im, ALU.mult, r, [tn + "1"])
        TT(eng, t2, a_im, b_re, ALU.mult, r, [tn + "2"])
        TT(eng, o_im, t1, t2, ALU.add, [tn + "1", tn + "2"], w)

    with ExitStack() as es:
        T = lambda name, shape, dt=F32: es.enter_context(nc.sbuf_tensor(name, shape, dt))
        SW = 1408
        stg = [T("stg0", [128, SW]), T("stg1", [128, SW])]
        wkvb = T("wkvb", [128, 8, 512], BF16); memT = T("memT", [128, 8, 256], BF16)
        GC = 4
        if do_ssm:
            sm = T("sm", [128, 30, G]); Bt = T("Bt", [128, 2, G * 16]); Ct = T("Ct", [128, 2, G * 16])
            dcs = T("dcs", [128, G]); PC = T("PC", [128, 2, G, 9]); RR = T("RR", [128, 2, G, 17]); PB = T("PB", [128, 2, G, 15])
            BB = T("BB", [128, 2, G, 16]); XY = T("XY", [128, 6, G, 16])
            ETre0 = T("ETre0", [128, G // 4, 128]); ETim0 = T("ETim0", [128, G // 4, 128])
            tA = T("tA", [128, 960]); tB = T("tB", [128, 960]); tC = T("tC", [128, 512]); tE = T("tE", [128, 384]); tF = T("tF", [128, 384])
            EBst = T("EBst", [128, GC, 15, 16]); EBsw = T("EBsw", [128, GC, 8, 16]); ECst = T("ECst", [128, GC, 9, 16])
            ETre1 = EBst[:].rearrange("p g i h -> p (g i h)")[:, 0:768].rearrange("p (g m) -> p g m", m=128)
            ETim1 = tA[:, 0:768].rearrange("p (g m) -> p g m", m=128)
            sm_ = lambda i: sm[:, i, :]
            sn = lambda i: f"sm{i}"

            def tt(o, a, b, alu, eng="dve"):
                TT(eng, sm_(o), sm_(a), sm_(b), alu, [sn(a), sn(b)], [sn(o)])

            def ts(o, a, s1, s2, o0, o1=ALU.bypass):
                TS("dve", sm_(o), sm_(a), s1, s2, o0, o1, [sn(a)], [sn(o)])

            ldgrp[0] = "c2"
            ld(sm_(0), lre, "sm0"); ld(sm_(1), lim, "sm1"); ld(sm_(2), ldt, "sm2")
            ld(Bt[:, 0, :], bre, "Bt"); ld(Bt[:, 1, :], bim, "Bt")
            ld(Ct[:, 0, :], cre, "Ct"); ld(Ct[:, 1, :], cim, "Ct"); ld(dcs[:], dcol, "dcs")
            ACT(sm_(3), sm_(2), AF.Exp, ["sm2"], ["sm3"])
            tt(4, 0, 3, ALU.mult); tt(5, 1, 3, ALU.mult)
            ACT(sm_(6), sm_(5), AF.Sin, ["sm5"], ["sm6"], scale=1.0 / 8)
            ACT(sm_(7), sm_(5), AF.Sin, ["sm5"], ["sm7"], scale=1.0 / 16)
            ACT(sm_(12), sm_(4), AF.Exp, ["sm4"], ["sm12"])
            ACT(sm_(15), sm_(4), AF.Exp, ["sm4"], ["sm15"], scale=-1.0)
            ACT(sm_(24), sm_(4), AF.Exp, ["sm4"], ["sm24"], scale=-8.0)
            ACT(rho8[:], sm_(4), AF.Exp, ["sm4"], ["rho8"], scale=8.0)

        pc = [0]

        def wload(dst, src, ncol, scale_ap=None):
            s_ = pc[0] % 2; pc[0] += 1
            S.dma("sp", stg[s_][:, 0:ncol], src, writes=[f"stg{s_}"], group=f"stg{s_}")
            if scale_ap is None:
                ACT(dst, stg[s_][:, 0:ncol], AF.Copy, [f"stg{s_}"], ["wts"])
            else:
                ACT(dst, stg[s_][:, 0:ncol], AF.Copy, [f"stg{s_}", "prm"], ["wts"], scale=scale_ap)
        for k in range(8):
            for hh in range(2):
                wload(win[:, k, hh * SW:(hh + 1) * SW], w_in[k * 128:(k + 1) * 128, hh * SW:(hh + 1) * SW], SW, prm[:, k:k + 1])
        for k in range(8):
            wload(wkvb[:, k, :], w_kv[k * 128:(k + 1) * 128, :], 512, prm[:, 8 + k:9 + k])
        for k in range(8):
            wload(wout[:, k, :], w_out[k * 128:(k + 1) * 128, :], 1024)
        for k in range(3):
            wload(wglu[:, k, :], w_glu[k * 128:(k + 1) * 128, :], 384)

        if do_ssm:
            tt(9, 7, 7, ALU.mult); ts(8, 9, -2.0, 1.0, ALU.mult, ALU.add)
            for _ in range(3):
                tt(9, 8, 8, ALU.mult); tt(10, 6, 6, ALU.mult); tt(11, 8, 6, ALU.mult)
                tt(8, 9, 10, ALU.subtract); ts(6, 11, 2.0, None, ALU.mult)
            tt(13, 12, 8, ALU.mult); tt(14, 12, 6, ALU.mult)
            tt(16, 15, 8, ALU.mult); tt(25, 15, 6, ALU.mult); ts(17, 25, -1.0, None, ALU.mult)
            for tab, bre_, bim_, nd in ((PC, 13, 14, 3), (RR, 16, 17, 4)):
                tn = "PC" if tab is PC else "RR"
                op("dve", lambda e, tab=tab: e.memset(tab[:, 0, :, 0], 1.0), [], [tn])
                op("dve", lambda e, tab=tab: e.memset(tab[:, 1, :, 0], 0.0), [], [tn])
                CP("dve", tab[:, 0, :, 1], sm_(bre_), [sn(bre_)], [tn])
                CP("dve", tab[:, 1, :, 1], sm_(bim_), [sn(bim_)], [tn])
                for kk in range(nd):
                    n = 1 << kk
                    tv = lambda t_: t_[:, 0:G * n].rearrange("p (g m) -> p g m", m=n)
                    cmul("dve", tab[:, 0, :, 1 + n:1 + 2 * n], tab[:, 1, :, 1 + n:1 + 2 * n],
                         tab[:, 0, :, 1:1 + n], tab[:, 1, :, 1:1 + n],
                         tab[:, 0, :, n:n + 1].to_broadcast([128, G, n]), tab[:, 1, :, n:n + 1].to_broadcast([128, G, n]),
                         tv(tA), tv(tB), [tn], [tn], "tAB")
            tv15 = lambda t_: t_[:, 0:G * 15].rearrange("p (g m) -> p g m", m=15)
            cmul("dve", PB[:, 0], PB[:, 1], RR[:, 0, :, 0:15], RR[:, 1, :, 0:15],
                 PC[:, 0, :, 7:8].to_broadcast([128, G, 15]), PC[:, 1, :, 7:8].to_broadcast([128, G, 15]),
                 tv15(tA), tv15(tB), ["PC", "RR"], ["PB"], "tAB")
            ts(18, 13, -1.0, None, ALU.add)
            tt(9, 18, 0, ALU.mult); tt(10, 14, 1, ALU.mult); tt(20, 9, 10, ALU.add)
            tt(9, 14, 0, ALU.mult); tt(10, 18, 1, ALU.mult); tt(21, 9, 10, ALU.subtract)
            tt(9, 0, 0, ALU.mult); tt(10, 1, 1, ALU.mult); tt(19, 9, 10, ALU.add)
            op("dve", lambda e: e.reciprocal(sm_(26), sm_(19)), ["sm19"], ["sm26"])
            tt(22, 20, 26, ALU.mult); tt(23, 21, 26, ALU.mult)
            b3 = lambda ap: ap.unsqueeze(2).to_broadcast([128, G, 16])
            B3 = lambda i: Bt[:, i, :].rearrange("p (g h) -> p g h", h=16)
            C3 = lambda i: Ct[:, i, :].rearrange("p (g h) -> p g h", h=16)
            v3 = lambda t_: t_[:, 0:G * 16].rearrange("p (g h) -> p g h", h=16)
            cmul("dve", BB[:, 0], BB[:, 1], b3(sm_(22)), b3(sm_(23)), B3(0), B3(1), v3(tA), v3(tB),
                 ["sm22", "sm23", "Bt"], ["BB"], "tAB")
            mT, mB, mBn, mTn = sel[:, 0:1], sel[:, 1:2], sel[:, 2:3], sel[:, 4:5]

            def stack(dst, top, top_m, bot, bot_m, r):
                TS("dve", dst, top, top_m, None, ALU.mult, ALU.bypass, r + ["sel"], ["XY"])
                STT("dve", dst, bot, bot_m, dst, ALU.mult, ALU.add, r + ["sel", "XY"], ["XY"])
            stack(XY[:, 0], BB[:, 0], mT, BB[:, 1], mB, ["BB"])
            stack(XY[:, 1], BB[:, 1], mTn, BB[:, 0], mB, ["BB"])
            stack(XY[:, 2], BB[:, 1], mT, BB[:, 0], mB, ["BB"])
            stack(XY[:, 3], BB[:, 0], mT, BB[:, 1], mBn, ["BB"])
            stack(XY[:, 4], C3(0), mT, C3(1), mBn, ["Ct"])
            stack(XY[:, 5], C3(1), mTn, C3(0), mBn, ["Ct"])
            TT("dve", sm_(28), PC[:, 0, :, 8], sm_(24), ALU.mult, ["PC", "sm24"], ["sm28"])
            TT("dve", sm_(27), PC[:, 1, :, 8], sm_(24), ALU.mult, ["PC", "sm24"], ["sm27"])
            ts(29, 27, -1.0, None, ALU.mult)
            GH = G // 4
            for gh in range(3):
                gsl = slice(gh * GH, (gh + 1) * GH)
                en = "pool"
                ETre, ETim = (ETre0, ETim0)
                trn, tin = ("ETre0", "ETim0")
                tE_, tF_, tnm = (tE, tF, "tEF")
                CP(en, ETre[:, :, 0], sm[:, 28, gsl], ["sm28"], [trn])
                CP(en, ETim[:, :, 0], sm[:, 29, gsl], ["sm29"], [tin])
                for kk in range(7):
                    n = 1 << kk
                    tv = lambda t_: t_[:, 0:GH * n].rearrange("p (g m) -> p g m", m=n)
                    cmul(en, ETre[:, :, n:2 * n], ETim[:, :, n:2 * n], ETre[:, :, 0:n], ETim[:, :, 0:n],
                         ETre[:, :, n - 1:n].to_broadcast([128, GH, n]), ETim[:, :, n - 1:n].to_broadcast([128, GH, n]),
                         tv(tE_), tv(tF_), [trn, tin], [trn, tin], tnm)
                TS(en, ESp[:, gsl, :].rearrange("p g m -> p (g m)"), ETim[:].rearrange("p g m -> p (g m)"), sel[:, 3:4],
                   0.0, ALU.mult, ALU.add, [tin, "sel"], ["ESp"])
                CP(en, ECt[:, gsl, :].rearrange("p g m -> p (g m)"), ETre[:].rearrange("p g m -> p (g m)"), [trn], ["ECt"])
            for c in range(G // GC):
                gs = slice(GC * c, GC * c + GC)
                v4 = lambda t_, ni: t_[:, 0:GC * ni * 16].rearrange("p (g i h) -> p g i h", g=GC, i=ni)

                def prod(dst, tab, lo, ni, xa, xb_, res):
                    sh = [128, GC, ni, 16]
                    TT("dve", v4(tA, ni), tab[:, 0, gs, lo:lo + ni].unsqueeze(3).to_broadcast(sh),
                       XY[:, xa, gs, :].unsqueeze(2).to_broadcast(sh), ALU.mult, ["PB", "PC", "XY"], ["tAB1"])
                    TT("dve", v4(tB, ni), tab[:, 1, gs, lo:lo + ni].unsqueeze(3).to_broadcast(sh),
                       XY[:, xb_, gs, :].unsqueeze(2).to_broadcast(sh), ALU.mult, ["PB", "PC", "XY"], ["tAB2"])
                    TT("dve", dst, v4(tA, ni), v4(tB, ni), ALU.add, ["tAB1", "tAB2"], [res])
                prod(EBst[:], PB, 0, 15, 0, 1, "EBst")
                prod(ECst[:], PC, 0, 9, 4, 5, "ECst")
                CP("dve", Wc[:, gs, :].rearrange("p g (t h) -> p g t h", t=8), ECst[:, :, 1:9, :], ["ECst"], ["Wc"])
                W = GC * 128
                for gl in range(GC):
                    TR(pA[:, gl * 128:(gl + 1) * 128], EBst[:, gl, 0:8, :], identf[:], ["EBst", "identf"], ["pA0"])
                CP("dve", Wl[:, gs, :], pA[:, 0:W].rearrange("p (g n) -> p g n", g=GC), ["pA0"], ["Wl"])
                CP("dve", Wls[:, gs, 0:64], Wl[:, gs, 64:128], ["Wl"], ["Wls"])
                CP("dve", Wls[:, gs, 64:128], Wl[:, gs, 0:64], ["Wl"], ["Wls"])
                for gl in range(GC):
                    MM(pC[:, gl * 128:(gl + 1) * 128], EBst[:, gl, 7:15, :], ECst[:, gl, 0:8, :], True, True,
                       ["EBst", "ECst"], ["pC0"])
                m3 = lambda t_: t_[:, 0:W].rearrange("p (g n) -> p g n", g=GC)
                TT("dve", m3(tA), pC[:, 0:W].rearrange("p (g n) -> p g n", g=GC),
                   maskf[:].unsqueeze(1).to_broadcast([128, GC, 128]), ALU.mult, ["pC0", "maskf"], ["tAB1"])
                TT("dve", m3(tC), identf[:].unsqueeze(1).to_broadcast([128, GC, 128]),
                   dcs[:, gs].unsqueeze(2).to_broadcast([128, GC, 128]), ALU.mult, ["identf", "dcs"], ["tC"])
                TT("dve", MI[:, gs, :], m3(tA), m3(tC), ALU.add, ["tAB1", "tC"], ["MI"])
            for gh in range(3, 4):
                gsl = slice(gh * GH, (gh + 1) * GH)
                en = "dve"
                ETre, ETim = (ETre1, ETim1)
                trn, tin = ("EBst", "tAB1")
                tE_, tF_, tnm = (tB[:, 0:384], tB[:, 384:768], "tAB2_")
                CP(en, ETre[:, :, 0], sm[:, 28, gsl], ["sm28"], [trn])
                CP(en, ETim[:, :, 0], sm[:, 29, gsl], ["sm29"], [tin])
                for kk in range(7):
                    n = 1 << kk
                    tv = lambda t_: t_[:, 0:GH * n].rearrange("p (g m) -> p g m", m=n)
                    cmul(en, ETre[:, :, n:2 * n], ETim[:, :, n:2 * n], ETre[:, :, 0:n], ETim[:, :, 0:n],
                         ETre[:, :, n - 1:n].to_broadcast([128, GH, n]), ETim[:, :, n - 1:n].to_broadcast([128, GH, n]),
                         tv(tE_), tv(tF_), [trn, tin], [trn, tin], tnm)
                TS(en, ESp[:, gsl, :].rearrange("p g m -> p (g m)"), ETim[:].rearrange("p g m -> p (g m)"), sel[:, 3:4],
                   0.0, ALU.mult, ALU.add, [tin, "sel"], ["ESp"])
                CP(en, ECt[:, gsl, :].rearrange("p g m -> p (g m)"), ETre[:].rearrange("p g m -> p (g m)"), [trn], ["ECt"])
        act_only[0] = True
        pre_tiles([(mem[mc * 128:(mc + 1) * 128, :], memT[:, :, mc * 128:(mc + 1) * 128], "memT") for mc in range(2)])
        act_only[0] = False
        for fc in range(2):
            for k in range(8):
                MM(pA[:, 512:768], wkvb[:, k, fc * 128:(fc + 1) * 128], memT[:, k, :], k == 0, k == 7,
                   ["wts", "memT"], ["pA1"])
            CP("act", kT[:, fc, :], pA[:, 512:768], ["pA1"], ["kT"])
        for mc in range(2):
            for k in range(8):
                MM(pB[:, 512:768], memT[:, k, mc * 128:(mc + 1) * 128], wkvb[:, k, 256:512], k == 0, k == 7,
                   ["wts", "memT"], ["pB1"])
            CP("act", vv[:, mc, :], pB[:, 512:768], ["pB1"], ["vv"])
        if do_ssm and stage > 1:
            pre_early([(xp[t * 128:(t + 1) * 128, :],) for t in range(4)], "p0")
        S.barrier()

    if stage <= 1:
        S.dma("sp", out[0:128, 0:256], gfs[:, 0:256], reads=["gfs"], writes=[], group="st")
        S.final_wait("st"); S.finish(); return nc
    hnT = A("hnT", [128, 8, 2 + 1024], BF16)
    Asb = A("Asb", [128, 1024], BF16); U8 = A("U8", [128, G, 128], BF16)
    b1 = A("b1", [128, 1024], BF16); b2 = A("b2", [128, 1024], BF16)
    rbf = A("rbf", [128, 1024], BF16); Sbuf = A("Sbuf", [128, G, 129], BF16)
    gT = A("gT", [128, 3, 1024], BF16)
    rend = A("rend", [128, 8], BF16); rsw = A("rsw", [128, 8], BF16); t3e = A("t3e", [128, 8], F32); t4e = A("t4e", [128, 8], F32)
    sa = A("sa", [128, 3, 512], BF16); sc = A("sc", [128, 3, 512], BF16); cbs = A("cbs", [128, 3, 512], BF16)
    ci3 = A("ci", [128, 3, 512], BF16); ub = A("ub", [128, 3, 514], BF16)
    qT = A("qT", [128, 2, 512], BF16); sq = A("sq", [128, 2, 512], BF16); pp4 = A("pp", [128, 4, 512], BF16); pp = pp4
    yT = A("yT", [128, 8, 512], BF16); rden = A("rden", [128, 512], F32); tmpb = A("tmpb", [128, 512], BF16)
    op("dve", lambda e: e.memset(Sbuf[:, :, 128], 0.0), [], ["Sb00", "Sb01", "Sb10", "Sb11", "Sb20", "Sb21"])
    if not do_ssm:
        op("dve", lambda e: e.memset(gT[:].rearrange("p c n -> p (c n)"), 0.0), [], ["gT"])

    HN = [f"hn{t}" for t in range(8)]

    def blk_items(src, base):
        return [(src[base + t * 128: base + (t + 1) * 128, :], hnT[:, :, 2 + t * 128: 2 + (t + 1) * 128], HN[t])
                for t in range(8)]

    def pre_block(src, base, key=None):
        pre_tiles(blk_items(src, base), key)

    def ssm_block(with_out, fill=None, hook=None):
        def emit_fill(n):
            if fill is None:
                return
            for _ in range(n):
                u_ = next(fill, None)
                if u_ is None:
                    return
                u_()

        def F1(c):
            gs = slice(8 * c, 8 * c + 8)
            for j8 in range(8):
                for k in range(8):
                    MM(pA[:, j8 * 128:(j8 + 1) * 128], hnT[:, k, 2 + j8: 2 + 1024: 8], win[:, k, c * 128:(c + 1) * 128],
                       k == 0, k == 7, HN + ["wts"], ["pA0", "pA1"])
            CP("act", Asb[:].rearrange("p (g j h) -> p j g h", g=8, j=8),
               pA[:].rearrange("p (j g h) -> p j g h", j=8, g=8), ["pA0", "pA1"], ["Asb"])
            for gl in range(8):
                TR(pT0[:, gl * 128:(gl + 1) * 128], Asb[:, gl * 128:(gl + 1) * 128], identb[:], ["Asb", "identb"], ["pT0"])
            CP("act", U8[:, gs, :], pT0[:].rearrange("p (g n) -> p g n", g=8), ["pT0"], [f"U8_{c}"])

        def F2(c, h):
            for gl in range(4):
                g = 8 * c + 4 * h + gl
                MM(pB[:, gl * 128:(gl + 1) * 128], Wl[:, g, :], U8[:, g, :], True, True, ["Wl", f"U8_{c}"], ["pB0"])
            for gl in range(4):
                g = 8 * c + 4 * h + gl
                MM(pB[:, 512 + gl * 128:512 + (gl + 1) * 128], Wls[:, g, :], U8[:, g, :], True, True, ["Wls", f"U8_{c}"], ["pB1"])

        def BC(c, h):
            g0 = 8 * c + 4 * h
            gs = slice(g0, g0 + 4)
            fs = slice(h * 512, (h + 1) * 512)
            b1n, b2n, rn = f"b1{h}", f"b2{h}", f"rbf{h}"
            ec = ECt[:, gs, :].rearrange("p g m -> p (g m)"); esp = ESp[:, gs, :].rearrange("p g m -> p (g m)")
            TT("dve", b1[:, fs], pB[:, 0:512], ec, ALU.mult, ["pB0", "ECt"], [b1n])
            TT("dve", b2[:, fs], pB[:, 512:1024], esp, ALU.mult, ["pB1", "ESp"], [b2n])
            TT("dve", b1[:, fs], b1[:, fs], b2[:, fs], ALU.add, [b1n, b2n], [b1n])
            for gl in range(4):
                g = g0 + gl
                cs = slice(h * 512 + gl * 128, h * 512 + (gl + 1) * 128)
                op("dve", lambda e, cs=cs, g=g: e.tensor_tensor_scan(
                    rbf[:, cs], rho8[:, g:g + 1].to_broadcast([128, 128]), b1[:, cs], send[:, g:g + 1], ALU.mult, ALU.add),
                   [b1n, "rho8", f"send{c}{h}"], [rn])
            MM(pB[:, 0:512], permb[:], rbf[:, fs], True, True, ["permb", rn], ["pB0"])
            TT("dve", b1[:, fs], rbf[:, fs], ec, ALU.mult, [rn, "ECt"], [b1n])
            TT("dve", b2[:, fs], pB[:, 0:512], esp, ALU.mult, ["pB0", "ESp"], [b2n])
            CP("pool", Sbuf[:, gs, 0:1], Sbuf[:, gs, 128:129], [f"Sb{c}{h}"], [f"Sb{c}{h}"])
            TT("dve", Sbuf[:, gs, 1:129], b1[:, fs].rearrange("p (g m) -> p g m", g=4),
               b2[:, fs].rearrange("p (g m) -> p g m", g=4), ALU.subtract, [b1n, b2n, f"Sb{c}{h}"], [f"Sb{c}{h}"])
            TT("pool", send[:, gs], b1[:, h * 512 + 127:(h + 1) * 512:128], b2[:, h * 512 + 127:(h + 1) * 512:128],
               ALU.subtract, [b1n, b2n], [f"send{c}{h}"])

        def F2full(c):
            for gl in range(8):
                g = 8 * c + gl
                MM(pB[:, gl * 128:(gl + 1) * 128], Wl[:, g, :], U8[:, g, :], True, True, ["Wl", f"U8_{c}"], ["pB0", "pB1"])
            for gl in range(8):
                g = 8 * c + gl
                MM(pC[:, gl * 128:(gl + 1) * 128], Wls[:, g, :], U8[:, g, :], True, True, ["Wls", f"U8_{c}"], ["pC0", "pC1"])

        def BCfull(c):
            gs = slice(8 * c, 8 * c + 8)
            sbn = [f"Sb{c}0", f"Sb{c}1"]; sdn = [f"send{c}0", f"send{c}1"]
            ec = ECt[:, gs, :].rearrange("p g m -> p (g m)"); esp = ESp[:, gs, :].rearrange("p g m -> p (g m)")
            TT("dve", b1[:], pB[:], ec, ALU.mult, ["pB0", "pB1", "ECt"], ["b10", "b11"])
            TT("dve", b2[:], pC[:], esp, ALU.mult, ["pC0", "pC1", "ESp"], ["b20", "b21"])
            TT("dve", b1[:], b1[:], b2[:], ALU.add, ["b10", "b11", "b20", "b21"], ["b10", "b11"])
            for gl in range(8):
                g = 8 * c + gl
                op("dve", lambda e, gl=gl, g=g: e.tensor_tensor_scan(
                    rbf[:, gl * 128:(gl + 1) * 128], rho8[:, g:g + 1].to_broadcast([128, 128]),
                    b1[:, gl * 128:(gl + 1) * 128], send[:, g:g + 1], ALU.mult, ALU.add),
                   ["b10", "b11", "rho8"] + sdn, ["rbf0", "rbf1"])
            CP("pool", rend[:], rbf[:, 127:1024:128], ["rbf0", "rbf1"], ["rend"])
            S.dma("sp", rsw[64:128, :], rend[0:64, :], reads=["rend"], writes=["rswA"], group="rswA")
            S.dma("sp", rsw[0:64, :], rend[64:128, :], reads=["rend"], writes=["rswB"], group="rswB")
            TT("pool", t3e[:], rend[:], ECt[:, gs, 127], ALU.mult, ["rend", "ECt"], ["t3e"])
            TT("pool", t4e[:], rsw[:], ESp[:, gs, 127], ALU.mult, ["rswA", "rswB", "ESp"], ["t4e"])
            TT("pool", send[:, gs], t3e[:], t4e[:], ALU.subtract, ["t3e", "t4e"], sdn)
            CP("pool", Sbuf[:, gs, 128], send[:, gs], sdn + sbn, sbn)

        def Dst(c):
            for gl in range(8):
                g = 8 * c + gl
                MM(pA[:, gl * 128:(gl + 1) * 128], U8[:, g, :], MI[:, g, :], True, False, [f"U8_{c}", "MI"], ["pA0", "pA1"])
                MM(pA[:, gl * 128:(gl + 1) * 128], Sbuf[:, g, 0:128], Wc[:, g, :], False, True,
                   [f"Sb{c}0", f"Sb{c}1", "Wc"], ["pA0", "pA1"])
            ACT(Asb[:].rearrange("p (t g h) -> p g t h", t=8, g=8),
                pA[:].rearrange("p (g t h) -> p g t h", g=8, t=8), AF.Gelu_apprx_tanh, ["pA0", "pA1"], ["Asb"])
            for t8 in range(8):
                TR(pT1[:, t8 * 128:(t8 + 1) * 128], Asb[:, t8 * 128:(t8 + 1) * 128], identb[:], ["Asb", "identb"], ["pT1"])
            CP("act", gT[:, c, :], pT1[:], ["pT1"], ["gT"])

        if fill is None and not with_out:
            F1(0); F2full(0); F1(1); BCfull(0); F2full(1); F1(2)
            if hook is not None:
                hook()
            BCfull(1); F2full(2); BCfull(2)
            return
        F1(0); F2(0, 0); emit_fill(3 if with_out else 0)
        for c in range(3):
            BC(c, 0); F2(c, 1); emit_fill(3 if c == 0 else 2)
            if c + 1 < 3:
                F1(c + 1); emit_fill(2 if c == 0 else 1)
            BC(c, 1)
            if c + 1 < 3:
                F2(c + 1, 0); emit_fill(2)
            if with_out:
                Dst(c); emit_fill(1)
        emit_fill(1000)

    bank_sets = {"all": [(pA, 0, "pA0"), (pA, 1, "pA1"), (pB, 0, "pB0"), (pB, 1, "pB1")],
                 "pc": [(pC, 0, "pC0"), (pC, 1, "pC1")]}
    pwc = [0]

    def mixer_units(blk, first, bs):
        banks = bank_sets[bs]
        c0 = 2 + blk * 512
        cols = slice(c0, c0 + 512)
        hres = HN[4 * blk:4 * blk + 4]

        def nextbank():
            t_, h_, nm_ = banks[pwc[0] % len(banks)]; pwc[0] += 1
            return t_[:, h_ * 512:(h_ + 1) * 512], nm_

        def projb(fi):
            bank, nm_ = nextbank()
            for k in range(8):
                MM(bank, win[:, k, fi * 128:(fi + 1) * 128], hnT[:, k, cols], k == 0, k == 7, hres + ["wts"], [nm_])
            return bank, nm_
        units = []
        if first:
            units.append(lambda: pre_tiles([(xh, tmp_halo, "pp0")]))

        def silu_unit(fi, dst, res):
            def u_():
                b_, n_ = projb(fi)
                ACT(dst, b_, AF.Silu, [n_], [res])
            return u_
        for i in range(3):
            units.append(silu_unit(3 + i, sa[:, i, :], "sa"))
        for i in range(3):
            units.append(silu_unit(15 + i, sc[:, i, :], "sc"))
        for i in range(2):
            units.append(silu_unit(20 + i, sq[:, i, :], "sq"))

        def conv_in(i):
            ci = ci3[:, i, :]
            cin = f"ci{i}"

            def u1():
                if first:
                    bh, nh = nextbank()
                    for k in range(8):
                        MM(bh[:, 0:2], win[:, k, (6 + i) * 128:(7 + i) * 128], tmp_halo[:, k, 0:2], k == 0, k == 7,
                           ["pp0", "wts"], [nh])
                    CP("act", ci[:, 0:2], bh[:, 0:2], [nh], [cin])
                    bh2, nh2 = nextbank()
                    for k in range(8):
                        MM(bh2[:, 0:2], win[:, k, (12 + i) * 128:(13 + i) * 128], tmp_halo[:, k, 0:2], k == 0, k == 7,
                           ["pp0", "wts"], [nh2])
                    TT("dve", ub[:, i, 0:2], bh2[:, 0:2], ci[:, 0:2], ALU.mult, [nh2, cin], [f"ub{i}"])
                else:
                    CP("pool", ub[:, i, 0:2], ub[:, i, 512:514], [f"ub{i}"], [f"ub{i}"])
                b_, n_ = projb(6 + i)
                CP("act", ci, b_, [n_], [cin])

            def u2():
                b_, n_ = projb(12 + i)
                CP("act", ub[:, i, 2:514], b_, [n_], [f"ub{i}"])

            def u2b():
                TT("dve", ub[:, i, 2:514], ub[:, i, 2:514], ci, ALU.mult, [f"ub{i}", cin], [f"ub{i}"])

            def u3():
                b_, n_ = projb(9 + i)
                CP("act", cbs[:, i, :], b_, [n_], ["cbs"])
            return [u1, u2, u3, u2b]
        late = []
        for i in range(3):
            u1_, u2_, u3_, u2b_ = conv_in(i)
            units += [u1_, u2_, u3_]
            late.append(u2b_)

        def q_unit(i):
            def u_():
                b_, n_ = projb(18 + i)
                CP("act", qT[:, i, :], b_, [n_], ["qT"])
            return u_
        for i in range(2):
            units.append(q_unit(i))

        def conv_unit(i):
            def u_():
                bank, nm_ = nextbank()
                for tap in range(3):
                    MM(bank, cdiag[:, i * 3 + tap, :], ub[:, i, tap:tap + 512], tap == 0, tap == 2, ["cdiag", f"ub{i}"], [nm_])
                STT("dve", tmpb[:], bank, prm[:, 22 + i:23 + i], cbs[:, i, :], ALU.add, ALU.mult,
                    [nm_, "prm", "cbs"], ["tmpb"])
                TT("dve", yT[:, 3 + i, :], tmpb[:], sc[:, i, :], ALU.mult, ["tmpb", "sc"], [f"yT{3 + i}"])
            return u_
        units += late
        for i in range(3):
            units.append(conv_unit(i))
        return units

    def glu(blk):
        gview = lambda c: gT[:, c, :].rearrange("p (j n) -> p j n", j=8)[:, :, blk * 64:(blk + 1) * 64]
        tmp2 = rden[:].bitcast(BF16)[:, 0:512]
        for oc in range(3):
            tb, tbn = (tmpb[:], "tmpb") if oc % 2 == 0 else (tmp2, "rden")
            t3v = tb.rearrange("p (j n) -> p j n", j=8)
            for c in range(3):
                MM(pC[:, (oc % 2) * 512:(oc % 2 + 1) * 512], wglu[:, c, oc * 128:(oc + 1) * 128], gview(c), c == 0, c == 2,
                   ["wts", "gT"], [f"pC{oc % 2}"])
            ACT(tb, pC[:, (oc % 2) * 512:(oc % 2 + 1) * 512], AF.Tanh, [f"pC{oc % 2}", "prm"], [tbn], scale=0.5,
                bias=prm[:, 19 + oc:20 + oc])
            STT("dve", t3v, t3v, 1.0, gview(oc), ALU.add, ALU.mult, [tbn, "gT"], [tbn])
            STT("dve", yT[:, oc, :].rearrange("p (n j) -> p n j", j=8), tb.rearrange("p (j n) -> p n j", j=8), 0.5,
                sa[:, oc, :].rearrange("p (n j) -> p n j", j=8), ALU.mult, ALU.mult, [tbn, "sa"], [f"yT{oc}"])

    def mixer_rest(hf, blk, do_glu=True):
        prefetch_x(hf, blk)
        if do_glu:
            glu(blk)
        sbank = {0: (pA, "pA"), 1: (pC, "pC")}

        def scores(h):
            pr, hh = divmod(h, 2)
            rows = slice(hh * 64, hh * 64 + 64)
            bt, bn = sbank[hh]
            for mc in range(2):
                MM(bt[:, mc * 512:(mc + 1) * 512], kT[rows, pr, mc * 128:(mc + 1) * 128], qT[rows, pr, :], True, True,
                   ["kT", "qT"], [f"{bn}{mc}"])

        def expo(h):
            hh = h % 2
            bt, bn = sbank[hh]
            for mc in range(2):
                ACT(pp4[:, 2 * hh + mc, :], bt[:, mc * 512:(mc + 1) * 512], AF.Exp, [f"{bn}{mc}"], [f"pp{hh}"], scale=0.125)

        def pv(h):
            pr, hh = divmod(h, 2)
            rows = slice(hh * 64, hh * 64 + 64)
            for mc in range(2):
                MM(pB[rows, 0:512], vv[:, mc, h * 64:(h + 1) * 64], pp4[:, 2 * hh + mc, :], mc == 0, mc == 1,
                   ["vv", f"pp{hh}"], ["pB0"])
            for mc in range(2):
                MM(pB[rows, 512:1024], onesb[:], pp4[:, 2 * hh + mc, :], mc == 0, mc == 1, ["onesb", f"pp{hh}"], ["pB1"])

        def fin(pr):
            ACT(rden[:], pB[:, 512:1024], AF.Ln, ["pB1"], ["rden"])
            ACT(rden[:], rden[:], AF.Exp, ["rden"], ["rden"], scale=-1.0)
            TT("dve", tmpb[:], pB[:, 0:512], rden[:], ALU.mult, ["pB0", "rden"], ["tmpb"])
            TT("dve", yT[:, 6 + pr, :], tmpb[:], sq[:, pr, :], ALU.mult, ["tmpb", "sq"], [f"yT{6 + pr}"])
        scores(0); expo(0); scores(1); expo(1); pv(0); scores(2); pv(1); fin(0)
        expo(2); scores(3); expo(3); pv(2); pv(3); fin(1)
        tiles = []
        for j in range(4):
            i = tilectr[0]; tilectr[0] += 1
            tiles.append((i % NXB, hf * 1024 + blk * 512 + j * 128))
        for j, (sl, tok) in enumerate(tiles):
            xb = xbuf[sl]; xn = f"xb{sl}"
            pO, pOn = (pC, "pC") if j % 2 == 0 else (pB, "pB")
            for nh in range(2):
                for kc in range(8):
                    MM(pO[:, nh * 512:(nh + 1) * 512], yT[:, kc, j * 128:(j + 1) * 128], wout[:, kc, nh * 512:(nh + 1) * 512],
                       kc == 0, kc == 7, [f"yT{kc}", "wts"], [f"{pOn}{nh}"])
            TT("dve", xb[:], pO[:], xb[:], ALU.add, [f"{pOn}0", f"{pOn}1", xn], [xn])
            ACT(junkD, xb[:], AF.Square, [xn], ["junk1", "ssqB"], accum_out=stat2[:, j:j + 1])
        ACT(stat2[:, 4:8], stat2[:, 0:4], AF.Ln, ["ssqB", "epsc"], ["vB"], scale=1.0 / D, bias=epsc[:, 0:1])
        ACT(stat2[:, 8:12], stat2[:, 4:8], AF.Exp, ["vB"], ["rsB"], scale=-0.5)
        for j, (sl, tok) in enumerate(tiles):
            xb = xbuf[sl]; xn = f"xb{sl}"
            STT("dve", xb[:], xb[:], stat2[:, 8 + j:9 + j], gfs[:], ALU.mult, ALU.mult, [xn, "rsB", "gfs"], [xn])
            S.dma("sp", out[tok:tok + 128, :], xb[:], reads=[xn], writes=[], group=f"st{sl}")

    def prefetch_x(hf, blk):
        for j in range(4):
            sl = (tilectr[0] + j) % NXB
            tok = hf * 1024 + blk * 512 + j * 128
            S.dma("sp", xbuf[sl][:], xs[tok:tok + 128, :], writes=[f"xb{sl}"], group=f"xb{sl}")

    tmp_halo = pp4[:, 0:2, :].rearrange("p a (b n) -> p (a b) n", b=4)

    if do_ssm:
        pre_block(xp, 0, "p0")
        ssm_block(False, hook=lambda: pre_early(blk_items(xp, 1024), "p1"))
        pre_block(xp, 1024, "p1")
        ssm_block(False, hook=lambda: pre_early(blk_items(xs, 0), "o0"))
    if stage == 2:
        S.dma("sp", out[0:128, 0:256], gfs[:, 0:256], reads=["Sbuf", "send", "gfs"], writes=[], group="st")
        S.final_wait("st"); S.finish(); return nc
    for hf in range(2):
        pre_block(xs, hf * 1024, "o0" if (hf == 0 and do_ssm) else None)
        u0 = mixer_units(0, hf == 0, "pc" if do_ssm else "all")
        if do_ssm:
            ssm_block(True, iter(u0))
        else:
            for u_ in u0:
                u_()
        mixer_rest(hf, 0)
        for iu, u_ in enumerate(mixer_units(1, False, "all")):
            u_()
            if iu == 2:
                glu(1)
        mixer_rest(hf, 1, do_glu=False)
    for i_ in range(NXB):
        S.final_wait(f"st{i_}")
    S.finish()
    return nc


def kernel(**inputs):
    f = lambda a: np.ascontiguousarray(np.asarray(a, dtype=np.float32))
    x = f(inputs["x"]); mem = f(inputs["mem"])
    t128 = lambda a: np.ascontiguousarray(np.tile(a, (2,) + (1,) * (a.ndim - 1)))
    lam_re = f(inputs["ssm_lam_re"])[0]; lam_im = f(inputs["ssm_lam_im"])[0]
    shared = {
        "w_in": f(inputs["w_in"])[0], "w_out": f(inputs["w_out"])[0],
        "w_glu": f(inputs["w_glu"])[0], "w_kv": f(inputs["w_mem_kv"])[0],
        "gin": f(f(inputs["norm_g"])[0].reshape(8, 128).T),
        "gmem": f(f(inputs["mem_norm_g"])[0].reshape(8, 128).T),
        "gfin": f(np.tile(f(inputs["final_norm_g"]).reshape(1, D), (128, 1))),
        "bglu": f(f(inputs["b_glu"])[0].reshape(3, 128).T),
        "cw": f(f(inputs["conv_w"])[0].reshape(3, 3, 128).transpose(2, 1, 0).reshape(128, 9)),
        "cbias": f(f(inputs["conv_b"])[0].reshape(3, 128).T),
        "lre": t128(f(lam_re.T)), "lim": t128(f(lam_im.T)),
        "ldt": f(np.tile(f(inputs["ssm_log_dt"])[0].reshape(1, G), (128, 1))),
        "bre": t128(f(f(inputs["ssm_b_re"])[0].transpose(1, 0, 2).reshape(64, G * 16))),
        "bim": t128(f(f(inputs["ssm_b_im"])[0].transpose(1, 0, 2).reshape(64, G * 16))),
        "cre": t128(f(f(inputs["ssm_c_re"])[0].transpose(2, 0, 1).reshape(64, G * 16))),
        "cim": t128(f(f(inputs["ssm_c_im"])[0].transpose(2, 0, 1).reshape(64, G * 16))),
        "dcol": f(np.tile(f(inputs["ssm_d"])[0].reshape(G, 16).T, (8, 1))),
        "ident": np.eye(128, dtype=np.float32),
        "mask": f(np.kron(np.triu(np.ones((8, 8), np.float32)), np.ones((16, 16), np.float32))),
        "perm": f(np.roll(np.eye(128, dtype=np.float32), 64, axis=1)),
        "sel": f(np.stack([np.r_[np.ones(64), np.zeros(64)], np.r_[np.zeros(64), np.ones(64)],
                           np.r_[np.zeros(64), -np.ones(64)], np.r_[-np.ones(64), np.ones(64)],
                           np.r_[-np.ones(64), np.zeros(64)], np.r_[np.ones(64), np.ones(64)]], axis=1)),
    }
    in_maps = []
    for b in range(4):
        for s in range(2):
            m = dict(shared)
            m["xs"] = f(x[b, s * NT:(s + 1) * NT])
            m["xp"] = f(x[b, 0:NT]) if s == 1 else np.zeros((NT, D), np.float32)
            xh = np.zeros((128, D), np.float32)
            if s == 1:
                xh[0:2] = x[b, NT - 2:NT]
            m["xh"] = xh
            m["mem"] = f(mem[b])
            in_maps.append(m)
    nc = build_nc(stage=STAGE)
    res = run_bass_kernel_spmd(nc, in_maps, core_ids=list(range(8)))
    outp = np.empty((4, 2 * NT, D), np.float32)
    for b in range(4):
        for s in range(2):
            outp[b, s * NT:(s + 1) * NT] = res.results[2 * b + s]["out"]
    return outp
```
